# Optimizing a Trainium2 kernel written in Bass

```python
import math
import jax
import jax.numpy as jnp
from jax import lax
import numpy as np

D_MODEL = 2048
BATCH = 8
SEQ = 2048
DEPTH = 2

GRID_W = 64
CTX_LEN = 256
N_MIXERS = 2
N_RWKV = (DEPTH + 1) // 2
N_HYENA = DEPTH // 2
HEAD_SIZE = 64
N_HEADS = D_MODEL // HEAD_SIZE
DECAY_LORA = max(32, int(round(D_MODEL ** 0.5 * 1.8 / 32)) * 32)
ICLR_LORA = max(32, int(round(D_MODEL ** 0.5 * 1.8 / 32)) * 32)
GATE_LORA = max(32, int(round(D_MODEL ** 0.8 * 0.6 / 32)) * 32)
LNX_EPS = 64e-5
HY_ORDER = 2
HY_EMB_DIM = 33
HY_FILTER_WIDTH = 64
HY_INNER_MLPS = 2
HY_FAST_DECAY = 0.3
HY_SLOW_DECAY = 1.5
HY_DECAY_TARGET = 1e-2
D_FF = 4 * D_MODEL
N_MOD = 6
NORM_EPS = 1e-6

kernel_name = 'hybrid_rwkv7_hyena_prefix_dit'


def rms_norm(x, g):
    xf = x.astype(jnp.float32)
    y = xf * lax.rsqrt(jnp.mean(xf * xf, axis=-1, keepdims=True) + NORM_EPS)
    return (y * g.astype(jnp.float32)).astype(x.dtype)


def modulate(h, shift, scale):
    return h * (1.0 + scale) + shift


def to_heads(t):
    return t.reshape(t.shape[:-1] + (N_HEADS, HEAD_SIZE))


def shift_grid(x):
    B, L, D = x.shape
    rows = L // GRID_W
    q = D // 4
    p = jnp.pad(x.reshape(B, rows, GRID_W, D), ((0, 0), (1, 1), (1, 1), (0, 0)))
    s = jnp.concatenate([p[:, 1:-1, :-2, :q], p[:, 1:-1, 2:, q:2 * q],
                         p[:, :-2, 1:-1, 2 * q:3 * q], p[:, 2:, 1:-1, 3 * q:]], axis=-1)
    return s.reshape(B, L, D)


def shift_seq(x):
    half = x.shape[-1] // 2
    p = jnp.pad(x, ((0, 0), (1, 1), (0, 0)))
    return jnp.concatenate([p[:, :-2, :half], p[:, 2:, half:]], axis=-1)


def rwkv_keys(h, xx, mu, w_k, w_v, dec_w0, dec_w1, dec_w2, a0, a1, a2, k_k, k_a):
    f32 = jnp.float32
    B, L, D = h.shape
    hs = (B, L, N_HEADS, HEAD_SIZE)
    xw = h + xx * mu[1]
    xk = h + xx * mu[2]
    xv = h + xx * mu[3]
    xa = h + xx * mu[4]
    k = (xk @ w_k).astype(f32)
    v = (xv @ w_v).astype(f32)
    lw = dec_w0[:, None, None, :] + jnp.einsum('eblr,erd->ebld', jnp.tanh(jnp.einsum('bld,edr->eblr', xw, dec_w1)), dec_w2)
    lw = -jax.nn.softplus(-lw.astype(f32)) - 0.5
    decay = jnp.exp(-jnp.exp(lw))
    a = jax.nn.sigmoid((a0[:, None, None, :] + jnp.einsum('eblr,erd->ebld', jnp.einsum('bld,edr->eblr', xa, a1), a2)).astype(f32))
    kk = (k * k_k).reshape(hs)
    kk = kk * lax.rsqrt(jnp.maximum(jnp.sum(kk * kk, axis=-1, keepdims=True), 1e-24))
    kd = k * (1.0 + (a - 1.0) * k_a)
    b = kk * a.reshape((2,) + hs)
    return decay.reshape((2,) + hs), kd.reshape((2,) + hs), v.reshape(hs), kk, b


def wkv_scan(state, r, decay, k, v, kk, b, reverse):
    emit = r is not None
    seqs = (decay, k, v, kk, b) + ((r,) if emit else ())
    xs = tuple(jnp.moveaxis(s.astype(jnp.float32), 1, 0) for s in seqs)

    def step(S, inp):
        w_t, k_t, v_t, kk_t, b_t = inp[:5]
        S = (S * w_t[:, :, None, :]
             - jnp.einsum('bhvk,bhk->bhv', S, kk_t)[..., None] * b_t[:, :, None, :]
             + v_t[..., None] * k_t[:, :, None, :])
        out = jnp.einsum('bhvk,bhk->bhv', S, inp[5]) if emit else None
        return S, out

    S, out = lax.scan(step, state, xs, reverse=reverse)
    return S, (jnp.moveaxis(out, 0, 1) if emit else None)


def rwkv_readout(h, xx, r, o, kd, v, mu, g1, g2, r_k, lnx_w, lnx_b, w_o):
    B, L, D = h.shape
    mean = jnp.mean(o, axis=-1, keepdims=True)
    var = jnp.mean(jnp.square(o - mean), axis=-1, keepdims=True)
    o = ((o - mean) * lax.rsqrt(var + LNX_EPS)).reshape(B, L, D) * lnx_w + lnx_b
    bonus = jnp.sum(r[None] * kd * r_k, axis=(0, -1))[..., None] * v
    g = jax.nn.sigmoid((h + xx * mu[5]) @ g1) @ g2
    return ((o + bonus.reshape(B, L, D)).astype(h.dtype) * g) @ w_o


def rwkv_mixer(h, hc, emit_ctx, mu, w_r, w_k, w_v, w_o, dec_w0, dec_w1, dec_w2, a0, a1, a2,
               g1, g2, k_k, k_a, r_k, lnx_w, lnx_b):
    keyp = (mu, w_k, w_v, dec_w0, dec_w1, dec_w2, a0, a1, a2, k_k, k_a)
    readp = (mu, g1, g2, r_k, lnx_w, lnx_b, w_o)
    xx = shift_grid(h) - h
    xxc = shift_seq(hc) - hc
    dec, kd, v, kk, b = rwkv_keys(h, xx, *keyp)
    decc, kdc, vc, kkc, bc = rwkv_keys(hc, xxc, *keyp)
    r = to_heads((h + xx * mu[0]) @ w_r)
    rc = to_heads((hc + xxc * mu[0]) @ w_r) if emit_ctx else None
    S0 = jnp.zeros((h.shape[0], N_HEADS, HEAD_SIZE, HEAD_SIZE), jnp.float32)
    S_f, oc_f = wkv_scan(S0, rc, decc[0], kdc[0], vc, kkc, bc[0], False)
    S_b, oc_b = wkv_scan(S0, rc, decc[1], kdc[1], vc, kkc, bc[1], True)
    _, o_f = wkv_scan(S_f, r, dec[0], kd[0], v, kk, b[0], False)
    _, o_b = wkv_scan(S_b, r, dec[1], kd[1], v, kk, b[1], True)
    y = rwkv_readout(h, xx, r, o_f + o_b, kd, v, *readp)
    yc = rwkv_readout(hc, xxc, rc, oc_f + oc_b, kdc, vc, *readp) if emit_ctx else None
    return y, yc


def hyena_filters(L, D, f_w1, f_w23, f_w4, f_b, f_freq):
    f32 = jnp.float32
    t = jnp.linspace(0.0, 1.0, L, dtype=f32)[:, None]
    bands = (HY_EMB_DIM - 1) // 2
    freqs = jnp.linspace(1e-4, bands - 1, bands, dtype=f32)[None, :]
    ang = (2.0 * math.pi / L) * jnp.arange(L, dtype=f32)[:, None] * freqs
    z = jnp.concatenate([t, jnp.cos(ang), -jnp.sin(ang)], axis=-1)
    f_b = f_b.astype(f32)
    f_freq = f_freq.astype(f32)
    z = jnp.sin(f_freq[0] * (z @ f_w1.astype(f32) + f_b[0]))
    for m in range(HY_INNER_MLPS):
        z = jnp.sin(f_freq[m + 1] * (z @ f_w23[m].astype(f32) + f_b[m + 1]))
    filt = (z @ f_w4.astype(f32)).reshape(L, 2, HY_ORDER, D)
    max_decay = math.log(HY_DECAY_TARGET) / HY_FAST_DECAY
    min_decay = math.log(HY_DECAY_TARGET) / HY_SLOW_DECAY
    deltas = jnp.abs(jnp.linspace(min_decay, max_decay, D, dtype=f32))
    window = jnp.exp(-t * deltas)
    return filt * window[:, None, None, :]


def fft_long_conv(u, h_fwd, h_bwd):
    L = u.shape[1]
    kern = jnp.concatenate([h_fwd, jnp.zeros_like(h_fwd[:1]), h_bwd[:0:-1]], axis=0)
    U = jnp.fft.rfft(u, n=2 * L, axis=1)
    K = jnp.fft.rfft(kern, axis=0)
    return jnp.fft.irfft(U * K, n=2 * L, axis=1)[:, :L]


def hyena_stream(h, in_w, in_b, conv_w, conv_b, f_w1, f_w23, f_w4, f_b, f_freq, skip, out_w, out_b):
    B, L, D = h.shape
    z = h @ in_w + in_b
    p = jnp.pad(z, ((0, 0), (1, 1), (0, 0)))
    z = p[:, :-2] * conv_w[0] + p[:, 1:-1] * conv_w[1] + p[:, 2:] * conv_w[2] + conv_b
    v, x1, x2 = jnp.split(z, 3, axis=-1)
    filt = hyena_filters(L, D, f_w1, f_w23, f_w4, f_b, f_freq)
    skip = skip.astype(jnp.float32)
    y = v.astype(jnp.float32)
    for o, gate in enumerate((x1, x2)):
        y = gate.astype(jnp.float32) * (fft_long_conv(y, filt[:, 0, o], filt[:, 1, o]) + skip[o] * y)
    return y.astype(h.dtype) @ out_w + out_b


def squared_relu_mlp(h, w_up, w_down):
    return jnp.square(jax.nn.relu(h @ w_up)) @ w_down


def setup_inputs(seed: int = 0) -> dict:
    key = jax.random.key(seed)
    ks = iter(jax.random.split(key, 64))

    def nrm(shape, scale):
        return jax.random.normal(next(ks), shape, jnp.float32) * scale

    D, NA, NB = D_MODEL, N_RWKV, N_HYENA
    R, RA, RG, F, E = DECAY_LORA, ICLR_LORA, GATE_LORA, HY_FILTER_WIDTH, HY_EMB_DIM
    return {
        'x': nrm((BATCH, SEQ, D), 1.0),
        'c': nrm((BATCH, D), 1.0),
        'ctx': nrm((BATCH, CTX_LEN, D), 1.0),
        'c_ctx': nrm((D,), 1.0),
        'ada_w': nrm((DEPTH, D, N_MOD * D), D ** -0.5),
        'ada_b': nrm((DEPTH, N_MOD * D), 0.02),
        'norm_g': 1.0 + nrm((DEPTH, 4, D), 0.1),
        'mlp_up': nrm((DEPTH, D, D_FF), D ** -0.5),
        'mlp_down': nrm((DEPTH, D_FF, D), D_FF ** -0.5),
        'rw_mu': jax.random.uniform(next(ks), (NA, 6, D), jnp.float32),
        'rw_w_r': nrm((NA, D, D), D ** -0.5),
        'rw_w_k': nrm((NA, D, D), D ** -0.5),
        'rw_w_v': nrm((NA, D, D), D ** -0.5),
        'rw_w_o': nrm((NA, D, D), D ** -0.5),
        'rw_dec_w0': jnp.linspace(-6.0, -1.0, D, dtype=jnp.float32) + nrm((NA, 2, D), 0.3),
        'rw_dec_w1': nrm((NA, 2, D, R), D ** -0.5),
        'rw_dec_w2': nrm((NA, 2, R, D), 0.1 * R ** -0.5),
        'rw_a0': nrm((NA, 2, D), 0.3),
        'rw_a1': nrm((NA, 2, D, RA), D ** -0.5),
        'rw_a2': nrm((NA, 2, RA, D), 0.1 * RA ** -0.5),
        'rw_g1': nrm((NA, D, RG), D ** -0.5),
        'rw_g2': nrm((NA, RG, D), RG ** -0.5),
        'rw_k_k': 0.85 + nrm((NA, D), 0.05),
        'rw_k_a': 1.0 + nrm((NA, D), 0.05),
        'rw_r_k': nrm((NA, N_HEADS, HEAD_SIZE), 0.1),
        'rw_lnx_w': 1.0 + nrm((NA, D), 0.1),
        'rw_lnx_b': nrm((NA, D), 0.02),
        'hy_in_w': nrm((NB, D, 3 * D), D ** -0.5),
        'hy_in_b': nrm((NB, 3 * D), 0.02),
        'hy_conv_w': nrm((NB, 3, 3 * D), 3 ** -0.5),
        'hy_conv_b': nrm((NB, 3 * D), 0.02),
        'hy_f_w1': nrm((NB, E, F), E ** -0.5),
        'hy_f_w23': nrm((NB, HY_INNER_MLPS, F, F), F ** -0.5),
        'hy_f_w4': nrm((NB, F, 2 * HY_ORDER * D), F ** -0.5),
        'hy_f_b': nrm((NB, HY_INNER_MLPS + 1, F), 0.1),
        'hy_f_freq': 1.0 + nrm((NB, HY_INNER_MLPS + 1, F), 0.1),
        'hy_skip': nrm((NB, HY_ORDER, D), 1.0),
        'hy_out_w': nrm((NB, D, D), D ** -0.5),
        'hy_out_b': nrm((NB, D), 0.02),
    }


def reference(x, c, ctx, c_ctx, ada_w, ada_b, norm_g, mlp_up, mlp_down,
              rw_mu, rw_w_r, rw_w_k, rw_w_v, rw_w_o, rw_dec_w0, rw_dec_w1, rw_dec_w2,
              rw_a0, rw_a1, rw_a2, rw_g1, rw_g2, rw_k_k, rw_k_a, rw_r_k, rw_lnx_w, rw_lnx_b,
              hy_in_w, hy_in_b, hy_conv_w, hy_conv_b, hy_f_w1, hy_f_w23, hy_f_w4, hy_f_b,
              hy_f_freq, hy_skip, hy_out_w, hy_out_b):
    B, _, D = x.shape
    silu_c = jax.nn.silu(c)
    silu_cc = jax.nn.silu(c_ctx)
    xc = ctx
    for i in range(DEPTH):
        kind, j = i % N_MIXERS, i // N_MIXERS
        ctx_live = any(l % N_MIXERS == 0 for l in range(i + 1, DEPTH))
        mod = (silu_c @ ada_w[i] + ada_b[i]).reshape(B, N_MOD, 1, D)
        modc = (silu_cc @ ada_w[i] + ada_b[i]).reshape(N_MOD, D)
        h = modulate(rms_norm(x, norm_g[i, 0]), mod[:, 0], mod[:, 1])
        if kind == 0:
            hc = modulate(rms_norm(xc, norm_g[i, 0]), modc[0], modc[1])
            y, yc = rwkv_mixer(h, hc, ctx_live, rw_mu[j], rw_w_r[j], rw_w_k[j], rw_w_v[j], rw_w_o[j],
                               rw_dec_w0[j], rw_dec_w1[j], rw_dec_w2[j], rw_a0[j], rw_a1[j], rw_a2[j],
                               rw_g1[j], rw_g2[j], rw_k_k[j], rw_k_a[j], rw_r_k[j],
                               rw_lnx_w[j], rw_lnx_b[j])
        else:
            hyp = (hy_in_w[j], hy_in_b[j], hy_conv_w[j], hy_conv_b[j], hy_f_w1[j], hy_f_w23[j],
                   hy_f_w4[j], hy_f_b[j], hy_f_freq[j], hy_skip[j], hy_out_w[j], hy_out_b[j])
            y = hyena_stream(h, *hyp)
            if ctx_live:
                hc = modulate(rms_norm(xc, norm_g[i, 0]), modc[0], modc[1])
                yc = hyena_stream(hc, *hyp)
        x = x + mod[:, 2] * rms_norm(y, norm_g[i, 1])
        h = modulate(rms_norm(x, norm_g[i, 2]), mod[:, 3], mod[:, 4])
        x = x + mod[:, 5] * rms_norm(squared_relu_mlp(h, mlp_up[i], mlp_down[i]), norm_g[i, 3])
        if ctx_live:
            xc = xc + modc[2] * rms_norm(yc, norm_g[i, 1])
            hc = modulate(rms_norm(xc, norm_g[i, 2]), modc[3], modc[4])
            xc = xc + modc[5] * rms_norm(squared_relu_mlp(hc, mlp_up[i], mlp_down[i]), norm_g[i, 3])
    return x
```

```python
import numpy as np
import ml_dtypes
from contextlib import ExitStack
import concourse.bass as bass
import concourse.mybir as mybir
from concourse.bass_utils import run_bass_kernel_spmd

F32 = mybir.dt.float32
BF16 = mybir.dt.bfloat16
AF = mybir.ActivationFunctionType
ALU = mybir.AluOpType
AX = mybir.AxisListType

T = 2048
D = 2048
TC = 256
TT = 2560
NH = 32
DFF = 8192
NCORES = 8


class Res:
    __slots__ = ("n", "excl")

    def __init__(self, n="", excl=False):
        self.n = n
        self.excl = excl


class Op:
    __slots__ = ("eng", "fn", "deps", "dma", "seq", "sem", "val")


class Sched:
    NSLOT = 8
    COMPUTE = ("pe", "dve", "act", "pool")
    QUEUES = ("sp", "pool", "act")

    def __init__(self, nc, stack):
        self.nc = nc
        self.sems = []
        self.esem = {}
        for e in self.COMPUTE:
            self.esem[e] = len(self.sems)
            self.sems.append(stack.enter_context(nc.semaphore("es_" + e)))
        self.dsem = {}
        for q in self.QUEUES:
            self.dsem[q] = []
            for i in range(self.NSLOT):
                self.dsem[q].append(len(self.sems))
                self.sems.append(stack.enter_context(nc.semaphore("ds_%s%d" % (q, i))))
        self.ecount = {e: 0 for e in self.COMPUTE}
        self.dcount = {q: 0 for q in self.QUEUES}
        self.known = {e: {} for e in ("pe", "dve", "act", "pool", "sp")}
        self.first = True
        self.reset_phase()

    def reset_phase(self):
        self.ops = []
        self.lastw = {}
        self.readers = {}

    def add(self, eng, fn, r=(), w=(), dma=False):
        ex = [k for k in r if getattr(k, "excl", False)]
        if ex:
            r = [k for k in r if not getattr(k, "excl", False)]
            w = list(w) + ex
        deps = set()
        for k in r:
            k = id(k)
            if k in self.lastw:
                deps.add(self.lastw[k])
        for k in w:
            k = id(k)
            if k in self.lastw:
                deps.add(self.lastw[k])
            deps.update(self.readers.get(k, ()))
        idx = len(self.ops)
        op = Op()
        op.eng = eng
        op.fn = fn
        op.deps = deps
        op.dma = dma
        if dma:
            k = self.dcount[eng]
            self.dcount[eng] += 1
            op.sem = self.dsem[eng][k % self.NSLOT]
            op.val = 16 * (k // self.NSLOT + 1)
            op.seq = None
        else:
            self.ecount[eng] += 1
            op.seq = self.ecount[eng]
            op.sem = self.esem[eng]
            op.val = op.seq
        self.ops.append(op)
        for k in r:
            self.readers.setdefault(id(k), []).append(idx)
        for k in w:
            self.lastw[id(k)] = idx
            self.readers[id(k)] = []
        return idx

    def dma(self, q, out, in_, r=(), w=()):
        return self.add(q, lambda e: e.dma_start(out=out, in_=in_), r, w, dma=True)

    def _wait(self, e, en, sem, val):
        if self.known[en].get(sem, 0) < val:
            e.wait_ge(self.sems[sem], val)
            self.known[en][sem] = val

    def _emit(self, e, en, op):
        waits = {}
        for d in op.deps:
            dop = self.ops[d]
            if (not dop.dma) and dop.eng == en and en == "pe":
                continue
            waits[dop.sem] = max(waits.get(dop.sem, 0), dop.val)
        if op.dma and op.val > 16:
            waits[op.sem] = max(waits.get(op.sem, 0), op.val - 16)
        for sem, val in waits.items():
            self._wait(e, en, sem, val)
        ins = op.fn(e)
        ins.then_inc(self.sems[op.sem], 16 if op.dma else 1)

    def flush(self):
        nc = self.nc
        ops = self.ops
        byeng = {}
        for i, op in enumerate(ops):
            byeng.setdefault(op.eng, []).append(i)
        first = self.first
        self.first = False
        with nc.Block() as block:
            deco = {"sp": block.sync, "dve": block.vector, "act": block.scalar,
                    "pool": block.gpsimd, "pe": block.tensor}

            def make(en):
                def body(e):
                    for i in byeng.get(en, []):
                        self._emit(e, en, ops[i])
                    if en == "sp":
                        for q in self.QUEUES:
                            k = self.dcount[q]
                            for s in range(self.NSLOT):
                                n = (k - s + self.NSLOT - 1) // self.NSLOT
                                if n > 0:
                                    self._wait(e, en, self.dsem[q][s], 16 * n)
                return body

            for en in ("sp", "dve", "act", "pool", "pe"):
                if en == "sp" or en in byeng:
                    deco[en](make(en))
        self.reset_phase()


class Ctx:
    n = 0

    def __init__(self, nc, S):
        self.nc = nc
        self.S = S
        self.st = ExitStack()

    def sb(self, shape, dt=F32, name=None):
        Ctx.n += 1
        t = self.st.enter_context(self.nc.sbuf_tensor("%s_%d" % (name or "t", Ctx.n), list(shape), dt))
        return t

    def ps(self, shape=(128, 512), dt=F32, name=None):
        Ctx.n += 1
        t = self.st.enter_context(self.nc.psum_tensor("%s_%d" % (name or "p", Ctx.n), list(shape), dt))
        return t

    def close(self):
        self.S.flush()
        self.st.close()


def ap(t):
    return t.ap() if hasattr(t, "ap") and callable(t.ap) else t


def phase_clear(nc, S):
    with nc.Block() as block:
        @block.sync
        def _(e):
            for s in S.sems:
                e.sem_clear(s)


def phase_ada(nc, S, cc, ada_w, ada_b, modout):
    C = Ctx(nc, S)
    s_raw = C.sb([128, 16, 2])
    s = C.sb([128, 16, 2])
    wt = [C.sb([128, 4096]) for _ in range(3)]
    ps = [C.ps() for _ in range(8)]
    acc = [C.sb([2, 4096]) for _ in range(2)]
    bt = [C.sb([2, 4096]) for _ in range(2)]
    S.dma("sp", s_raw[:], cc, w=[s_raw])
    S.add("act", lambda e: e.activation(out=s[:], in_=s_raw[:], func=AF.Silu), r=[s_raw], w=[s])
    cnt = 0
    gi = 0
    for l in range(2):
        for g in range(3):
            b_ = bt[gi % 2]
            a_ = acc[gi % 2]
            gi += 1
            S.dma("act", b_[:], ada_b[l, g * 4096:(g + 1) * 4096].partition_broadcast(2), w=[b_])
            for kc in range(16):
                w = wt[cnt % 3]
                cnt += 1
                S.dma("sp", w[:], ada_w[l, kc * 128:(kc + 1) * 128, g * 4096:(g + 1) * 4096], w=[w])
                for nb in range(8):
                    S.add("pe", (lambda nb=nb, kc=kc, w=w: lambda e: e.matmul(
                        ps[nb][0:2, :], lhsT=s[:, kc, :], rhs=w[:, nb * 512:(nb + 1) * 512],
                        start=(kc == 0), stop=(kc == 15)))(), r=[s, w], w=[ps[nb]])
            for nb in range(8):
                S.add("dve", (lambda nb=nb, a_=a_, b_=b_: lambda e: e.tensor_tensor(
                    out=a_[:, nb * 512:(nb + 1) * 512], in0=ps[nb][0:2, :],
                    in1=b_[:, nb * 512:(nb + 1) * 512], op=ALU.add))(), r=[ps[nb], b_], w=[a_])
            S.dma("pool", modout[l, :, g * 4096:(g + 1) * 4096], a_[:], r=[a_], w=[])
    C.close()


COLS = ["k_k", "k_a", "r_k", "lnx_w", "lnx_b", "dec_w0_0", "dec_w0_1", "a0_0", "a0_1",
        "hy_in_b0", "hy_in_b1", "hy_in_b2", "cw0_0", "cw0_1", "cw0_2", "cw1_0", "cw1_1", "cw1_2",
        "cw2_0", "cw2_1", "cw2_2", "cb_0", "cb_1", "cb_2", "hy_out_b"]
CI = {n: i for i, n in enumerate(COLS)}


def L(f):
    return f


def phase_norm(nc, S, K, x_in, ntok, y_fm, x_out, h_fm, hbases, gm_rows, h_rows):
    C = Ctx(nc, S)
    NB = 2
    xt = [C.sb([128, D]) for _ in range(NB)]
    tmp = [C.sb([128, D]) for _ in range(NB)]
    junk = C.sb([128, D])
    ss4 = [C.sb([128, 4]) for _ in range(NB)]
    ssa = [C.sb([128, 1]) for _ in range(NB)]
    ssb = [C.sb([128, 1]) for _ in range(NB)]
    if y_fm is not None:
        yf = [C.sb([128, 16, 128]) for _ in range(NB)]
        xn = [C.sb([128, D]) for _ in range(NB)]
        pY = [C.ps() for _ in range(4)]
        gm = C.sb([128, D])
        gtmp = C.sb([128, D])
        S.dma("sp", gm[:], gm_rows[0].partition_broadcast(128), w=[gm])
        S.dma("sp", gtmp[:], gm_rows[1].partition_broadcast(128), w=[gtmp])
        S.add("dve", lambda e: e.tensor_tensor(out=gm[:], in0=gm[:], in1=gtmp[:], op=ALU.mult), r=[gm, gtmp], w=[gm])
        yv = y_fm.rearrange("(c p) t -> p c t", p=128)
    else:
        xn = xt
    if h_fm is not None:
        hb = [C.sb([128, D], BF16) for _ in range(NB)]
        hT = [C.sb([128, 16, 128], BF16) for _ in range(NB)]
        pH = C.ps([128, 2048], BF16)
        gg = C.sb([128, D])
        sh = C.sb([128, D])
        g2 = C.sb([128, D])
        S.dma("sp", g2[:], h_rows[0].partition_broadcast(128), w=[g2])
        S.dma("sp", gg[:], h_rows[1].partition_broadcast(128), w=[gg])
        S.dma("sp", sh[:], h_rows[2].partition_broadcast(128), w=[sh])
        S.add("dve", lambda e: e.scalar_tensor_tensor(out=gg[:], in0=gg[:], scalar=1.0, in1=g2[:],
                                                     op0=ALU.add, op1=ALU.mult), r=[gg, g2], w=[gg])
        hv = h_fm.rearrange("(c p) t -> p c t", p=128)

    def rstd(ss, i):
        S.add("dve", lambda e: e.tensor_scalar(out=ss[:], in0=ss[:], scalar1=1.0 / D, scalar2=1e-6,
                                               op0=ALU.mult, op1=ALU.add), r=[ss], w=[ss])
        S.add("act", lambda e: e.activation(out=ss[:], in_=ss[:], func=AF.Sqrt), r=[ss], w=[ss])
        S.add("dve", lambda e: e.reciprocal(out=ss[:], in_=ss[:]), r=[ss], w=[ss])

    for i in range(ntok // 128):
        b = i % NB
        x_, t_, s4, sa, sb_ = xt[b], tmp[b], ss4[b], ssa[b], ssb[b]
        S.dma("sp", x_[:], x_in[i * 128:(i + 1) * 128, :], w=[x_])
        if y_fm is not None:
            y_, xn_ = yf[b], xn[b]
            S.dma("act", y_[:], yv[:, :, i * 128:(i + 1) * 128], w=[y_])
            for c in range(16):
                S.add("pe", (lambda c=c, y_=y_: lambda e: e.transpose(
                    out=pY[c // 4][:, (c % 4) * 128:(c % 4 + 1) * 128], in_=y_[:, c, :], identity=K["identF"][:]))(),
                    r=[y_, K["identF"]], w=[pY[c // 4]])
            for j in range(4):
                S.add("act", (lambda j=j, s4=s4: lambda e: e.activation(
                    out=junk[:, j * 512:(j + 1) * 512], in_=pY[j][:], func=AF.Square, accum_out=s4[:, j:j + 1]))(),
                    r=[pY[j]], w=[junk, s4])
            S.add("dve", (lambda s4=s4, sa=sa: lambda e: e.tensor_reduce(out=sa[:], in_=s4[:], axis=AX.X, op=ALU.add))(),
                  r=[s4], w=[sa])
            rstd(sa, i)
            for j in range(4):
                S.add("dve", (lambda j=j, t_=t_, sa=sa: lambda e: e.scalar_tensor_tensor(
                    out=t_[:, j * 512:(j + 1) * 512], in0=pY[j][:], scalar=sa[:, 0:1], in1=gm[:, j * 512:(j + 1) * 512],
                    op0=ALU.mult, op1=ALU.mult))(), r=[pY[j], sa, gm], w=[t_])
            S.add("pool", (lambda t_=t_, x_=x_, xn_=xn_: lambda e: e.tensor_tensor(
                out=xn_[:], in0=t_[:], in1=x_[:], op=ALU.add))(), r=[t_, x_], w=[xn_])
            if x_out is not None:
                S.dma("pool", x_out[i * 128:(i + 1) * 128, :], xn_[:], r=[xn_])
        else:
            xn_ = x_
        if h_fm is not None:
            h_, hT_ = hb[b], hT[b]
            S.add("act", (lambda xn_=xn_, sb_=sb_: lambda e: e.activation(
                out=junk[:], in_=xn_[:], func=AF.Square, accum_out=sb_[:, 0:1]))(), r=[xn_], w=[junk, sb_])
            rstd(sb_, i)
            S.add("dve", (lambda t_=t_, xn_=xn_, sb_=sb_: lambda e: e.scalar_tensor_tensor(
                out=t_[:], in0=xn_[:], scalar=sb_[:, 0:1], in1=gg[:], op0=ALU.mult, op1=ALU.mult))(),
                r=[xn_, sb_, gg], w=[t_])
            S.add("pool", (lambda t_=t_, h_=h_: lambda e: e.tensor_tensor(
                out=h_[:], in0=t_[:], in1=sh[:], op=ALU.add))(), r=[t_, sh], w=[h_])
            for c in range(16):
                S.add("pe", (lambda c=c, h_=h_: lambda e: e.transpose(
                    out=pH[:, c * 128:(c + 1) * 128], in_=h_[:, c * 128:(c + 1) * 128], identity=K["identB"][:]))(),
                    r=[h_, K["identB"]], w=[pH])
            S.add("act", (lambda hT_=hT_: lambda e: e.activation(
                out=hT_[:, 0:8, :], in_=pH[:, 0:1024], func=AF.Copy))(), r=[pH], w=[hT_])
            S.add("dve", (lambda hT_=hT_: lambda e: e.tensor_copy(
                out=hT_[:, 8:16, :], in_=pH[:, 1024:2048]))(), r=[pH], w=[hT_])
            for hb0 in hbases:
                S.dma("pool", hv[:, :, hb0 + i * 128:hb0 + (i + 1) * 128], hT_[:], r=[hT_])
    C.close()


def phase_consts(nc, S, K, identF, identB, bones, cols, mu):
    S.dma("sp", K["identF"][:], identF, w=[K["identF"]])
    S.dma("sp", K["identB"][:], identB, w=[K["identB"]])
    S.dma("sp", K["bones"][:], bones, w=[K["bones"]])
    S.dma("sp", K["cols"][:], cols, w=[K["cols"]])
    S.dma("sp", K["mu"][:], mu, w=[K["mu"]])
    S.add("dve", lambda e: e.tensor_scalar(out=K["omu"][:], in0=K["mu"][:], scalar1=-1.0, scalar2=1.0,
                                           op0=ALU.mult, op1=ALU.add), r=[K["mu"]], w=[K["omu"]])
    S.flush()


def phase_mix(nc, S, K, h_fm, xm):
    C = Ctx(nc, S)
    NB = 2
    hch = [C.sb([128, TT], BF16) for _ in range(NB)]
    hs = [C.sb([128, TT], BF16) for _ in range(NB)]
    tmp = [C.sb([128, TT], BF16) for _ in range(3)]
    outs = [C.sb([128, TT], BF16) for _ in range(3)]
    k = 0
    for c in range(16):
        b = c % NB
        h_, s_ = hch[b], hs[b]
        S.dma("sp", h_[:], h_fm[c * 128:(c + 1) * 128, :], w=[h_])
        S.add("pool", (lambda s_=s_: lambda e: e.memset(s_[:], 0.0))(), w=[s_])
        q = c // 4
        hl = h_[:, TC:TC + T].rearrange("p (r c) -> p r c", c=64)
        sl = s_[:, TC:TC + T].rearrange("p (r c) -> p r c", c=64)
        if q == 0:
            src, dst = hl[:, :, 0:63], sl[:, :, 1:64]
        elif q == 1:
            src, dst = hl[:, :, 1:64], sl[:, :, 0:63]
        elif q == 2:
            src, dst = hl[:, 0:31, :], sl[:, 1:32, :]
        else:
            src, dst = hl[:, 1:32, :], sl[:, 0:31, :]
        S.add("pool", (lambda src=src, dst=dst: lambda e: e.tensor_copy(out=dst, in_=src))(), r=[h_], w=[s_])
        for base in (0, TC + T):
            if c < 8:
                src, dst = h_[:, base:base + TC - 1], s_[:, base + 1:base + TC]
            else:
                src, dst = h_[:, base + 1:base + TC], s_[:, base:base + TC - 1]
            S.add("pool", (lambda src=src, dst=dst: lambda e: e.tensor_copy(out=dst, in_=src))(), r=[h_], w=[s_])
        for j in range(6):
            t_, o_ = tmp[k % 3], outs[k % 3]
            k += 1
            S.add("act", (lambda t_=t_, s_=s_, c=c, j=j: lambda e: e.activation(
                out=t_[:], in_=s_[:], func=AF.Copy, scale=K["mu"][:, c, j:j + 1]))(), r=[s_, K["mu"]], w=[t_])
            S.add("dve", (lambda t_=t_, o_=o_, h_=h_, c=c, j=j: lambda e: e.scalar_tensor_tensor(
                out=o_[:], in0=h_[:], scalar=K["omu"][:, c, j:j + 1], in1=t_[:], op0=ALU.mult, op1=ALU.add))(),
                r=[h_, t_, K["omu"]], w=[o_])
            S.dma("pool" if j % 2 else "act", xm[j][c * 128:(c + 1) * 128, :], o_[:], r=[o_])
    C.close()


def gemm_fm(nc, S, C, X, Kdim, NT, W, N, epi, msz=128, TB=None, xbf=True, nps=8):
    TB = TB or NT
    KC = (Kdim + 127) // 128
    ksz = [min(128, Kdim - kc * 128) for kc in range(KC)]
    Xs = C.sb([128, KC, TB], BF16, "Xs")
    wf = [C.sb([128, KC, msz], F32, "wf") for _ in range(2)]
    wb = [C.sb([128, KC, msz], BF16, "wb") for _ in range(2)]
    acc = [C.sb([128, TB], F32, "acc") for _ in range(2)]
    ps = [C.ps() for _ in range(nps)]
    it = 0
    pi = 0
    nblk = [(o, min(512, TB - o)) for o in range(0, TB, 512)]
    for tb in range(NT // TB):
        if Kdim % 128 == 0:
            xv = X.rearrange("(c p) t -> p c t", p=128)
            half = KC // 2 if KC >= 2 else KC
            S.dma("sp", Xs[:, 0:half, :], xv[:, 0:half, tb * TB:(tb + 1) * TB], w=[Xs])
            if half < KC:
                S.dma("act", Xs[:, half:KC, :], xv[:, half:KC, tb * TB:(tb + 1) * TB], w=[Xs])
        else:
            S.dma("sp", Xs[0:Kdim, 0, :], X[:, tb * TB:(tb + 1) * TB], w=[Xs])
        for m in range(N // msz):
            wf_, wb_, acc_ = wf[it % 2], wb[it % 2], acc[it % 2]
            if Kdim % 128 == 0:
                if len(W.shape) == 4:
                    S.dma("sp", wf_[:], W[m], w=[wf_])
                else:
                    S.dma("sp", wf_[:], W[:, m * msz:(m + 1) * msz].rearrange("(c p) n -> p c n", p=128), w=[wf_])
                if it % 2 == 0:
                    S.add("dve", (lambda wf_=wf_, wb_=wb_: lambda e: e.tensor_copy(out=wb_[:], in_=wf_[:]))(),
                          r=[wf_], w=[wb_])
                else:
                    S.add("act", (lambda wf_=wf_, wb_=wb_: lambda e: e.activation(out=wb_[:], in_=wf_[:], func=AF.Copy))(),
                          r=[wf_], w=[wb_])
            else:
                S.dma("sp", wf_[0:Kdim, 0, :], W[:, m * msz:(m + 1) * msz], w=[wf_])
                S.add("pool", (lambda wf_=wf_, wb_=wb_: lambda e: e.tensor_copy(
                    out=wb_[0:Kdim, 0, :], in_=wf_[0:Kdim, 0, :]))(), r=[wf_], w=[wb_])
            for (o, n) in nblk:
                p_ = ps[pi % nps]
                pi += 1
                for kc in range(KC):
                    S.add("pe", (lambda p_=p_, wb_=wb_, kc=kc, o=o, n=n: lambda e: e.matmul(
                        p_[0:msz, 0:n], lhsT=wb_[0:ksz[kc], kc, :], rhs=Xs[0:ksz[kc], kc, o:o + n],
                        start=(kc == 0), stop=(kc == KC - 1)))(), r=[wb_, Xs], w=[p_])
                if (pi % 2) == 0:
                    S.add("act", (lambda p_=p_, acc_=acc_, o=o, n=n: lambda e: e.activation(
                        out=acc_[0:msz, o:o + n], in_=p_[0:msz, 0:n], func=AF.Copy))(), r=[p_], w=[acc_])
                else:
                    S.add("dve", (lambda p_=p_, acc_=acc_, o=o, n=n: lambda e: e.tensor_copy(
                        out=acc_[0:msz, o:o + n], in_=p_[0:msz, 0:n]))(), r=[p_], w=[acc_])
            epi(tb, m, acc_, it)
            it += 1


def epi_store(S, dst, msz=128, TB=None):
    def epi(tb, m, acc, it):
        if TB is None:
            S.dma("pool", dst[m * msz:(m + 1) * msz, :], acc[0:msz, :], r=[acc])
        else:
            S.dma("pool", dst[m * msz:(m + 1) * msz, tb * TB:(tb + 1) * TB], acc[0:msz, :], r=[acc])
    return epi


def epi_act(S, C, dst, func, NT, odt=BF16, msz=128, bias=None, post=None):
    ot = [C.sb([128, NT], odt, "eo") for _ in range(2)]

    def epi(tb, m, acc, it):
        o_ = ot[it % 2]
        if bias is None:
            S.add("act", lambda e: e.activation(out=o_[0:msz, :], in_=acc[0:msz, :], func=func), r=[acc], w=[o_])
        else:
            bcol = bias(m)
            S.add("act", lambda e: e.activation(out=o_[0:msz, :], in_=acc[0:msz, :], func=func, bias=bcol),
                  r=[acc], w=[o_])
        if post is not None:
            S.add("dve", lambda e: e.tensor_scalar(out=o_[0:msz, :], in0=o_[0:msz, :], scalar1=post, scalar2=None,
                                                   op0=ALU.mult), r=[o_], w=[o_])
        S.dma("pool", dst[m * msz:(m + 1) * msz, :], o_[0:msz, :], r=[o_])
    return epi


def phase_rwkv_proj(nc, S, K, xm, P, Dr):
    col = lambda name: (lambda m: K["cols"][:, CI[name], m:m + 1])
    for (j, wname, dst) in ((0, "rw_w_r", "r"), (2, "rw_w_k", "k"), (3, "rw_w_v", "v")):
        C = Ctx(nc, S)
        gemm_fm(nc, S, C, xm[j], D, TT, P[wname], D, epi_store(S, Dr[dst]))
        C.close()
    for e in range(2):
        C = Ctx(nc, S)
        gemm_fm(nc, S, C, xm[1], D, TT, P["rw_dec_w1"][e], 96, epi_act(S, C, Dr["t1"], AF.Tanh, TT, msz=96), msz=96)
        C.close()
        C = Ctx(nc, S)
        gemm_fm(nc, S, C, Dr["t1"], 96, TT, P["rw_dec_w2"][e], D,
                epi_act(S, C, Dr["logw"][e], AF.Sigmoid, TT, odt=F32, bias=col("dec_w0_%d" % e), post=-0.6065306597126334))
        C.close()
        C = Ctx(nc, S)
        gemm_fm(nc, S, C, xm[4], D, TT, P["rw_a1"][e], 96, epi_act(S, C, Dr["t1"], AF.Copy, TT, msz=96), msz=96)
        C.close()
        C = Ctx(nc, S)
        gemm_fm(nc, S, C, Dr["t1"], 96, TT, P["rw_a2"][e], D,
                epi_act(S, C, Dr["a"][e], AF.Sigmoid, TT, odt=F32, bias=col("a0_%d" % e)))
        C.close()
    C = Ctx(nc, S)
    gemm_fm(nc, S, C, xm[5], D, TT, P["rw_g1"], 256, epi_act(S, C, Dr["t2"], AF.Sigmoid, TT))
    C.close()
    C = Ctx(nc, S)
    gemm_fm(nc, S, C, Dr["t2"], 256, TT, P["rw_g2"], D, epi_store(S, Dr["g"]))
    C.close()


def phase_scanprep(nc, S, K, Dr):
    C = Ctx(nc, S)
    kt, vt, rt = C.sb([128, TT]), C.sb([128, TT]), C.sb([128, TT])
    at = [C.sb([128, TT]) for _ in range(2)]
    kk, sq, rs, kkn = C.sb([128, TT]), C.sb([128, TT]), C.sb([128, TT]), C.sb([128, TT])
    kd = [C.sb([128, TT]) for _ in range(2)]
    bb = [C.sb([128, TT]) for _ in range(2)]
    tmp = C.sb([128, TT])
    ps = [C.ps() for _ in range(8)]
    col = lambda name, c: K["cols"][:, CI[name], c:c + 1]
    blk = [(o, 512) for o in range(0, TT, 512)]
    pi = 0
    for c in range(16):
        rows = slice(c * 128, (c + 1) * 128)
        S.dma("sp", kt[:], Dr["k"][rows, :], w=[kt])
        S.dma("act", vt[:], Dr["v"][rows, :], w=[vt])
        S.dma("sp", rt[:], Dr["r"][rows, :], w=[rt])
        S.dma("act", at[0][:], Dr["a"][0][rows, :], w=[at[0]])
        S.dma("sp", at[1][:], Dr["a"][1][rows, :], w=[at[1]])
        S.add("dve", (lambda c=c: lambda e: e.tensor_scalar(out=kk[:], in0=kt[:], scalar1=col("k_k", c), scalar2=None,
                                                          op0=ALU.mult))(), r=[kt, K["cols"]], w=[kk])
        S.add("act", lambda e: e.activation(out=sq[:], in_=kk[:], func=AF.Square), r=[kk], w=[sq])
        for (o, n) in blk:
            p_ = ps[pi % 8]
            pi += 1
            S.add("pe", (lambda p_=p_, o=o, n=n: lambda e: e.matmul(p_[:, 0:n], lhsT=K["bones"][:], rhs=sq[:, o:o + n],
                                                                    start=True, stop=True))(), r=[sq, K["bones"]], w=[p_])
            S.add("dve", (lambda p_=p_, o=o, n=n: lambda e: e.tensor_scalar(
                out=rs[:, o:o + n], in0=p_[:, 0:n], scalar1=1e-24, scalar2=None, op0=ALU.max))(), r=[p_], w=[rs])
        S.add("act", lambda e: e.activation(out=rs[:], in_=rs[:], func=AF.Sqrt), r=[rs], w=[rs])
        S.add("dve", lambda e: e.reciprocal(out=rs[:], in_=rs[:]), r=[rs], w=[rs])
        S.add("pool", lambda e: e.tensor_tensor(out=kkn[:], in0=kk[:], in1=rs[:], op=ALU.mult), r=[kk, rs], w=[kkn])
        S.dma("pool", Dr["kkn"][rows, :], kkn[:], r=[kkn])
        for e_ in range(2):
            S.add("dve", (lambda e_=e_, c=c: lambda e: e.tensor_scalar(
                out=kd[e_][:], in0=at[e_][:], scalar1=-1.0, scalar2=col("k_a", c), op0=ALU.add, op1=ALU.mult))(),
                r=[at[e_], K["cols"]], w=[kd[e_]])
            S.add("dve", (lambda e_=e_: lambda e: e.scalar_tensor_tensor(
                out=kd[e_][:], in0=kd[e_][:], scalar=1.0, in1=kt[:], op0=ALU.add, op1=ALU.mult))(),
                r=[kd[e_], kt], w=[kd[e_]])
            S.dma("pool", Dr["kd"][e_][rows, :], kd[e_][:], r=[kd[e_]])
            S.add("pool", (lambda e_=e_: lambda e: e.tensor_tensor(out=bb[e_][:], in0=kkn[:], in1=at[e_][:], op=ALU.mult))(),
                  r=[kkn, at[e_]], w=[bb[e_]])
            S.dma("pool", Dr["b"][e_][rows, :], bb[e_][:], r=[bb[e_]])
        S.add("pool", lambda e: e.tensor_tensor(out=tmp[:], in0=kd[0][:], in1=kd[1][:], op=ALU.add), r=[kd[0], kd[1]], w=[tmp])
        S.add("dve", (lambda c=c: lambda e: e.scalar_tensor_tensor(
            out=tmp[:], in0=tmp[:], scalar=col("r_k", c), in1=rt[:], op0=ALU.mult, op1=ALU.mult))(),
            r=[tmp, rt, K["cols"]], w=[tmp])
        for (o, n) in blk:
            p_ = ps[pi % 8]
            pi += 1
            S.add("pe", (lambda p_=p_, o=o, n=n: lambda e: e.matmul(p_[:, 0:n], lhsT=K["bones"][:], rhs=tmp[:, o:o + n],
                                                                    start=True, stop=True))(), r=[tmp, K["bones"]], w=[p_])
            S.add("dve", (lambda p_=p_, o=o, n=n: lambda e: e.tensor_tensor(
                out=sq[:, o:o + n], in0=p_[:, 0:n], in1=vt[:, o:o + n], op=ALU.mult))(), r=[p_, vt], w=[sq])
        S.dma("pool", Dr["bonus"][rows, :], sq[:], r=[sq])
    C.close()


def phase_scan(nc, S, K, Dr, Odram, scan_consts, TDT=F32, nsteps=18, lvl=9, dbgd=None):
    C = Ctx(nc, S)
    sb = C.sb
    NHP = 4
    names = ("logw", "kkn", "r", "b", "kd", "v")
    SETS = []
    for _ in range(2):
        d = {n: sb([128, NHP, 128]) for n in names}
        d.update(cs=sb([128, NHP, 128]), G1=sb([128, NHP, 128]), tot=sb([128, NHP, 1]), gamC=sb([128, NHP, 1]),
                 AR=sb([128, NHP, 256], BF16), Bt=sb([128, NHP, 128], BF16), Kt=sb([128, NHP, 128], BF16),
                 Bt32=sb([128, NHP, 128]), At_tm=sb([128, NHP, 128], BF16), Bh_tm=sb([128, NHP, 128], BF16),
                 Kh_tm=sb([128, NHP, 128], BF16), V_tm=sb([128, NHP, 128], BF16), Rp=sb([128, NHP, 128]),
                 ACt=sb([128, NHP, 64]), Of=sb([128, NHP, 128]),
                 rRp=[Res() for _ in range(NHP)], rAC=[Res() for _ in range(NHP)], rOf=[Res() for _ in range(NHP)])
        SETS.append(d)
    ST = [sb([128, 16, 64]) for _ in range(2)]
    rST = [[Res() for _ in range(16)] for _ in range(2)]
    mask01 = sb([128, NHP, 128])
    MASK = [sb([128, 256]) for _ in range(2)]
    ID2 = sb([128, 64])
    MD = [sb([128, 128]) for _ in range(2)]
    M1c = [sb([128, 128]) for _ in range(2)]
    M2c = [sb([128, 128]) for _ in range(2)]
    MTD = [sb([128, 128]) for _ in range(2)]
    G = 8
    W = []
    for _ in range(G):
        W.append({"Nm": sb([128, 128], TDT), "NTm": sb([128, 128], TDT), "N1": sb([128, 128], BF16), "N2": sb([128, 128], BF16),
                  "Z": sb([128, 128], BF16), "Za": sb([128, 128], BF16), "Zc": sb([128, 128], BF16), "Z1": sb([128, 128], BF16),
                  "Zw": sb([128, 128], BF16), "Xb": sb([128, 128], BF16), "Tm": [sb([128, 128], TDT) for _ in range(2)],
                  "P": [sb([128, 128], TDT) for _ in range(2)], "PT": [sb([128, 128], TDT) for _ in range(2)],
                  "Mbr": sb([128, 128], BF16), "Mka": sb([128, 128], BF16), "Mkr": sb([128, 128], BF16), "Nf": sb([128, 128], TDT),
                  "U1": sb([128, 64], BF16), "ApT": sb([128, 64], BF16)})
    banks = [C.ps() for _ in range(8)]
    PU = [Res(excl=True) for _ in range(8)]

    S.dma("sp", mask01[:], scan_consts["mask01"][:, 0:NHP, :], w=[mask01])
    for e_ in range(2):
        S.dma("sp", MASK[e_][:], scan_consts["mask"][e_], w=[MASK[e_]])
        S.dma("sp", MD[e_][:], scan_consts["md"][e_], w=[MD[e_]])
        S.dma("sp", M1c[e_][:], scan_consts["m1c"][e_], w=[M1c[e_]])
        S.dma("sp", M2c[e_][:], scan_consts["m2c"][e_], w=[M2c[e_]])
        S.dma("sp", MTD[e_][:], scan_consts["mtd"][e_], w=[MTD[e_]])
        S.add("pool", (lambda e_=e_: lambda e: e.memset(ST[e_][:], 0.0))(), w=rST[e_])
    S.dma("sp", ID2[:], scan_consts["id2"], w=[ID2])
    identF = K["identF"]
    flat = lambda t: t[:].rearrange("p a b -> p (a b)")
    BC = [128, NHP, 128]
    tcount = [0]

    def mm(out, lhsT, rhs, r, w, start=True, stop=True):
        S.add("pe", lambda e: e.matmul(out, lhsT=lhsT, rhs=rhs, start=start, stop=stop), r=r, w=w)

    def ev(eng, out, in0, in1, op, r, w):
        if in1 is None:
            if eng == "act":
                S.add("act", lambda e: e.activation(out=out, in_=in0, func=AF.Copy), r=r, w=w)
            else:
                S.add(eng, lambda e: e.tensor_copy(out=out, in_=in0), r=r, w=w)
        else:
            S.add(eng, lambda e: e.tensor_tensor(out=out, in0=in0, in1=in1, op=op), r=r, w=w)

    def prep(unit, d):
        s, e_, hh = unit
        c = s if e_ == 0 else 19 - s
        cols = slice(c * 128, (c + 1) * 128)
        srcs = {"logw": Dr["logw"][e_], "kkn": Dr["kkn"], "r": Dr["r"], "b": Dr["b"][e_], "kd": Dr["kd"][e_], "v": Dr["v"]}
        for qi, n in enumerate(names):
            S.dma("sp" if qi % 2 == 0 else "act", d[n][:],
                  srcs[n].rearrange("(c p) t -> p c t", p=128)[:, hh * NHP:(hh + 1) * NHP, cols], w=[d[n]])
        yield
        lw, kkn, r_, b_, kd_, v_ = (d[n] for n in names)
        cs, G1, tot, gamC, AR, Bt, Kt, Bt32 = d["cs"], d["G1"], d["tot"], d["gamC"], d["AR"], d["Bt"], d["Kt"], d["Bt32"]
        S.add("dve", lambda e: e.tensor_tensor_scan(out=flat(cs), data0=flat(mask01), data1=flat(lw), initial=0.0,
                                                    op0=ALU.mult, op1=ALU.add), r=[mask01, lw], w=[cs])
        yield
        if e_ == 1:
            S.add("pool", lambda e: e.tensor_copy(out=tot[:], in_=cs[:, :, 127:128]), r=[cs], w=[tot])
            S.add("pool", lambda e: e.tensor_tensor(out=G1[:], in0=lw[:], in1=cs[:], op=ALU.subtract), r=[lw, cs], w=[G1])
            yield
            S.add("pool", lambda e: e.tensor_tensor(out=cs[:], in0=G1[:], in1=tot[:].to_broadcast(BC), op=ALU.add),
                  r=[G1, tot], w=[cs])
            yield
        S.add("act", lambda e: e.activation(out=G1[:], in_=cs[:], func=AF.Exp), r=[cs], w=[G1])
        S.add("pool", lambda e: e.tensor_tensor(out=lw[:], in0=cs[:], in1=lw[:], op=ALU.subtract), r=[cs, lw], w=[lw])
        yield
        S.add("act", lambda e: e.activation(out=lw[:], in_=lw[:], func=AF.Exp), r=[lw], w=[lw])
        gsl = G1[:, :, 127:128] if e_ == 0 else G1[:, :, 0:1]
        S.add("pool", lambda e: e.tensor_copy(out=gamC[:], in_=gsl), r=[G1], w=[gamC])
        S.add("act", lambda e: e.activation(out=cs[:], in_=cs[:], func=AF.Exp, scale=-1.0), r=[cs], w=[cs])
        yield
        S.add("dve", lambda e: e.scalar_tensor_tensor(out=kkn[:], in0=kkn[:], scalar=-1.0, in1=lw[:],
                                                      op0=ALU.mult, op1=ALU.mult), r=[kkn, lw], w=[kkn])
        S.add("pool", lambda e: e.tensor_copy(out=AR[:, :, 0:128], in_=kkn[:]), r=[kkn], w=[AR])
        yield
        S.add("pool", lambda e: e.tensor_tensor(out=r_[:], in0=r_[:], in1=G1[:], op=ALU.mult), r=[r_, G1], w=[r_])
        S.add("act", lambda e: e.activation(out=AR[:, :, 128:256], in_=r_[:], func=AF.Copy), r=[r_], w=[AR])
        yield
        S.add("dve", lambda e: e.tensor_tensor(out=b_[:], in0=b_[:], in1=cs[:], op=ALU.mult), r=[b_, cs], w=[b_])
        S.add("act", lambda e: e.activation(out=Bt[:], in_=b_[:], func=AF.Copy), r=[b_], w=[Bt])
        S.add("pool", lambda e: e.tensor_copy(out=Bt32[:], in_=b_[:]), r=[b_], w=[Bt32])
        yield
        S.add("pool", lambda e: e.tensor_tensor(out=b_[:], in0=b_[:], in1=gamC[:].to_broadcast(BC), op=ALU.mult),
              r=[b_, gamC], w=[b_])
        yield
        S.add("pool", lambda e: e.tensor_tensor(out=kd_[:], in0=kd_[:], in1=cs[:], op=ALU.mult), r=[kd_, cs], w=[kd_])
        S.add("act", lambda e: e.activation(out=Kt[:], in_=kd_[:], func=AF.Copy), r=[kd_], w=[Kt])
        yield
        S.add("pool", lambda e: e.tensor_tensor(out=kd_[:], in0=kd_[:], in1=gamC[:].to_broadcast(BC), op=ALU.mult),
              r=[kd_, gamC], w=[kd_])
        yield
        for (src, dst) in ((kkn, d["At_tm"]), (b_, d["Bh_tm"]), (kd_, d["Kh_tm"]), (v_, d["V_tm"])):
            for g in range(NHP // 4):
                bk = tcount[0] % 8
                tcount[0] += 1
                for j in range(4):
                    hp = g * 4 + j
                    S.add("pe", (lambda src=src, hp=hp, bk=bk, j=j: lambda e: e.transpose(
                        out=banks[bk][:, j * 128:(j + 1) * 128], in_=src[:, hp, :], identity=identF[:]))(),
                        r=[src, identF], w=[PU[bk]])
                S.add("act", (lambda dst=dst, g=g, bk=bk: lambda e: e.activation(
                    out=dst[:, g * 4:(g + 1) * 4, :], in_=banks[bk][:, :], func=AF.Copy))(), r=[PU[bk]], w=[dst])
                yield

    def heads(unit, d, grp):
        s, e_, hh = unit
        c = s if e_ == 0 else 19 - s
        need_o = 2 <= c <= 17
        mk = MASK[e_]
        kkn, r_ = d["kkn"], d["r"]
        AR, Bt, Kt, Bt32 = d["AR"], d["Bt"], d["Kt"], d["Bt32"]
        At_tm, Bh_tm, Kh_tm, V_tm = d["At_tm"], d["Bh_tm"], d["Kh_tm"], d["V_tm"]
        Rp, ACt, Of, gamC = d["Rp"], d["ACt"], d["Of"], d["gamC"]
        HD = []
        for gi in range(G):
            hidx = grp * G + gi
            hp, half = hidx // 2, hidx % 2
            hs = slice(half * 64, (half + 1) * 64)
            HD.append(dict(hp=hp, gp=hh * NHP + hp, hs=hs, hc=hs, w=W[gi], bk=banks[gi], rb=[PU[gi]]))
        for h in HD:
            hp, hs, bk = h["hp"], h["hs"], h["bk"]
            mm(bk[:, 0:128], Bt32[hs, hp, :], kkn[hs, hp, :], [Bt32, kkn], h["rb"])
            mm(bk[:, 128:256], kkn[hs, hp, :], Bt32[hs, hp, :], [Bt32, kkn], h["rb"])
            mm(bk[:, 256:384], Bt[hs, hp, :], AR[hs, hp, 128:256], [Bt, AR], h["rb"])
            mm(bk[:, 384:512], Kt[hs, hp, :], AR[hs, hp, 0:128], [Kt, AR], h["rb"])
        yield False
        for h in HD:
            w_, bk = h["w"], h["bk"]
            ev("act", w_["Nf"][:], bk[:, 0:128], None, None, h["rb"], [w_["Nf"]])
            ev("dve", w_["NTm"][:], bk[:, 128:256], MTD[e_][:], ALU.mult, h["rb"] + [MTD[e_]], [w_["NTm"]])
            ev("dve", w_["Mbr"][:], bk[:, 256:384], mk[:, 128:256], ALU.mult, h["rb"] + [mk], [w_["Mbr"]])
            ev("dve", w_["Mka"][:], bk[:, 384:512], mk[:, 0:128], ALU.mult, h["rb"] + [mk], [w_["Mka"]])
        yield True
        for h in HD:
            w_ = h["w"]
            ev("pool", w_["Nm"][:], w_["Nf"][:], MD[e_][:], ALU.mult, [w_["Nf"], MD[e_]], [w_["Nm"]])
            ev("pool", w_["Tm"][0][:], w_["Nm"][:], identF[:], ALU.add, [w_["Nm"], identF], [w_["Tm"][0]])
            ev("pool", w_["N1"][:], w_["Nf"][:], M1c[e_][:], ALU.mult, [w_["Nf"], M1c[e_]], [w_["N1"]])
            ev("pool", w_["N2"][:], w_["Nf"][:], M2c[e_][:], ALU.mult, [w_["Nf"], M2c[e_]], [w_["N2"]])
            h["P"], h["PT"], h["T"] = w_["Nm"], w_["NTm"], w_["Tm"][0]
        yield True
        for lv in range(4):
            last = lv == 3
            for h in HD:
                bk = h["bk"]
                mm(bk[:, 0:128], h["P"][:], h["PT"][:], [h["P"], h["PT"]], h["rb"])
                if not last:
                    mm(bk[:, 128:256], h["PT"][:], h["P"][:], [h["P"], h["PT"]], h["rb"])
            yield False
            for h in HD:
                w_, bk = h["w"], h["bk"]
                Pn, PTn = w_["P"][lv % 2], w_["PT"][lv % 2]
                ev("act", PTn[:], bk[:, 0:128], None, None, h["rb"], [PTn])
                if not last:
                    ev("act", Pn[:], bk[:, 128:256], None, None, h["rb"], [Pn])
                h["P"], h["PT"] = Pn, PTn
            yield True
            for h in HD:
                mm(h["bk"][:, 256:384], h["PT"][:], h["T"][:], [h["PT"], h["T"]], h["rb"])
            yield False
            for h in HD:
                w_ = h["w"]
                Tn = w_["Xb"] if last else w_["Tm"][(lv + 1) % 2]
                ev("dve", Tn[:], h["bk"][:, 256:384], h["T"][:], ALU.add, h["rb"] + [h["T"]], [Tn])
                h["T"] = Tn
            yield True
        for h in HD:
            mm(h["bk"][:, 384:448], h["w"]["Mka"][:], V_tm[:, h["hp"], h["hc"]], [h["w"]["Mka"], V_tm], h["rb"])
        yield False
        for h in HD:
            w_ = h["w"]
            ev("act", w_["Z"][:, 0:64], h["bk"][:, 384:448], None, None, h["rb"], [w_["Z"]])
            ev("pool", w_["Z"][:, 64:128], At_tm[:, h["hp"], h["hc"]], None, None, [At_tm], [w_["Z"]])
        yield True

        def apply64(srck, dstk):
            for h in HD:
                mm(h["bk"][:, 0:128], h["T"][:], h["w"][srck][:], [h["T"], h["w"][srck]], h["rb"])
            yield False
            for h in HD:
                ev("act", h["w"]["Za"][:], h["bk"][:, 0:128], None, None, h["rb"], [h["w"]["Za"]])
            yield True
            for h in HD:
                mm(h["bk"][:, 128:256], h["w"]["N1"][:], h["w"]["Za"][:], [h["w"]["N1"], h["w"]["Za"]], h["rb"])
            yield False
            for h in HD:
                ev("act", h["w"]["Zc"][:], h["bk"][:, 128:256], None, None, h["rb"], [h["w"]["Zc"]])
            yield True
            for h in HD:
                mm(h["bk"][:, 256:384], h["T"][:], h["w"]["Zc"][:], [h["T"], h["w"]["Zc"]], h["rb"])
            yield False
            for h in HD:
                ev("dve", h["w"][dstk][:], h["bk"][:, 256:384], h["w"]["Za"][:], ALU.add,
                   h["rb"] + [h["w"]["Za"]], [h["w"][dstk]])
            yield True

        yield from apply64("Z", "Z1")
        for h in HD:
            mm(h["bk"][:, 384:512], h["w"]["N2"][:], h["w"]["Z1"][:], [h["w"]["N2"], h["w"]["Z1"]], h["rb"])
        yield False
        for h in HD:
            ev("act", h["w"]["Zw"][:], h["bk"][:, 384:512], None, None, h["rb"], [h["w"]["Zw"]])
        yield True
        yield from apply64("Zw", "Z")
        for h in HD:
            w_ = h["w"]
            ev("pool", w_["U1"][:], w_["Z"][:, 0:64], w_["Z1"][:, 0:64], ALU.add, [w_["Z"], w_["Z1"]], [w_["U1"]])
            ev("pool", w_["ApT"][:], w_["Z"][:, 64:128], w_["Z1"][:, 64:128], ALU.add, [w_["Z"], w_["Z1"]], [w_["ApT"]])
        yield True
        for h in HD:
            hp, hs, hc, w_, bk = h["hp"], h["hs"], h["hc"], h["w"], h["bk"]
            mm(bk[hs, 0:64], w_["ApT"][:], Bh_tm[:, hp, hc], [w_["ApT"], Bh_tm], h["rb"])
            if need_o:
                mm(bk[hs, 128:256], w_["ApT"][:], w_["Mbr"][:], [w_["ApT"], w_["Mbr"]], h["rb"])
                mm(bk[:, 384:512], Kt[hs, hp, :], AR[hs, hp, 128:256], [Kt, AR], h["rb"])
        yield False
        for h in HD:
            hp, hs, bk, w_ = h["hp"], h["hs"], h["bk"], h["w"]
            S.add("dve", (lambda bk=bk, hs=hs, hp=hp: lambda e: e.scalar_tensor_tensor(
                out=ACt[hs, hp, :], in0=ID2[hs, :], scalar=gamC[hs, hp, 0:1], in1=bk[hs, 0:64],
                op0=ALU.mult, op1=ALU.add))(), r=h["rb"] + [ID2, gamC], w=[d["rAC"][hp]])
            if need_o:
                ev("dve", Rp[hs, hp, :], bk[hs, 128:256], r_[hs, hp, :], ALU.add, h["rb"] + [r_], [d["rRp"][hp]])
                ev("dve", w_["Mkr"][:], bk[:, 384:512], mk[:, 128:256], ALU.mult, h["rb"] + [mk], [w_["Mkr"]])
        yield True
        for h in HD:
            hp, gp, hs, hc, w_, bk = h["hp"], h["gp"], h["hs"], h["hc"], h["w"], h["bk"]
            if need_o:
                pO = bk[hs, 256:384]
                mm(pO, w_["U1"][:], w_["Mbr"][:], [w_["U1"], w_["Mbr"]], h["rb"], start=True, stop=False)
                mm(pO, V_tm[:, hp, hc], w_["Mkr"][:], [V_tm, w_["Mkr"]], h["rb"], start=False, stop=False)
                mm(pO, ST[e_][hs, gp, :], Rp[hs, hp, :], [rST[e_][gp], d["rRp"][hp]], h["rb"], start=False, stop=True)
            pS = bk[hs, 64:128]
            mm(pS, Bh_tm[:, hp, hc], w_["U1"][:], [Bh_tm, w_["U1"]], h["rb"], start=True, stop=False)
            mm(pS, Kh_tm[:, hp, hc], V_tm[:, hp, hc], [Kh_tm, V_tm], h["rb"], start=False, stop=False)
            mm(pS, ACt[hs, hp, :], ST[e_][hs, gp, :], [d["rAC"][hp], rST[e_][gp]], h["rb"], start=False, stop=True)
        yield False
        for h in HD:
            hp, gp, hs, bk = h["hp"], h["gp"], h["hs"], h["bk"]
            if need_o:
                ev("act", Of[hs, hp, :], bk[hs, 256:384], None, None, h["rb"], [d["rOf"][hp]])
            ev("act", ST[e_][hs, gp, :], bk[hs, 64:128], None, None, h["rb"], [rST[e_][gp]])
        yield True

    units = [(s, e_, hh) for s in range(nsteps) for e_ in range(2) for hh in range(16 // NHP)]
    NG = (2 * NHP) // G
    for _ in prep(units[0], SETS[0]):
        pass
    for ui, u in enumerate(units):
        d = SETS[ui % 2]
        nxt = prep(units[ui + 1], SETS[(ui + 1) % 2]) if ui + 1 < len(units) else None
        for grp in range(NG if lvl >= 3 else 0):
            for safe in heads(u, d, grp):
                if safe and nxt is not None:
                    try:
                        next(nxt)
                        next(nxt)
                    except StopIteration:
                        nxt = None
        if nxt is not None:
            for _ in nxt:
                pass
        s, e_, hh = u
        c = s if e_ == 0 else 19 - s
        if 2 <= c <= 17 and lvl >= 7:
            S.dma("pool", Odram[e_].rearrange("(c p) t -> p c t", p=128)[:, hh * NHP:(hh + 1) * NHP, (c - 2) * 128:(c - 1) * 128],
                  d["Of"][:], r=d["rOf"])
    C.close()


def phase_readout(nc, S, K, Dr, Odram, XO):
    C = Ctx(nc, S)
    o0, o1, bn, gt = C.sb([128, T]), C.sb([128, T]), C.sb([128, T]), C.sb([128, T])
    mean, sq, rstd = C.sb([128, T]), C.sb([128, T]), C.sb([128, T])
    xo = [C.sb([128, T], BF16) for _ in range(2)]
    ps = [C.ps() for _ in range(8)]
    col = lambda name, c: K["cols"][:, CI[name], c:c + 1]
    pi = 0
    for c in range(16):
        rows = slice(c * 128, (c + 1) * 128)
        S.dma("sp", o0[:], Odram[0][rows, :], w=[o0])
        S.dma("act", o1[:], Odram[1][rows, :], w=[o1])
        S.dma("sp", bn[:], Dr["bonus"][rows, TC:TC + T], w=[bn])
        S.dma("act", gt[:], Dr["g"][rows, TC:TC + T], w=[gt])
        S.add("pool", lambda e: e.tensor_tensor(out=o0[:], in0=o0[:], in1=o1[:], op=ALU.add), r=[o0, o1], w=[o0])
        for nb in range(4):
            p_ = ps[pi % 8]
            pi += 1
            S.add("pe", (lambda p_=p_, nb=nb: lambda e: e.matmul(p_[:, :], lhsT=K["bones"][:], rhs=o0[:, nb * 512:(nb + 1) * 512],
                                                               start=True, stop=True))(), r=[o0, K["bones"]], w=[p_])
            S.add("dve", (lambda p_=p_, nb=nb: lambda e: e.tensor_scalar(
                out=mean[:, nb * 512:(nb + 1) * 512], in0=p_[:, :], scalar1=1.0 / 64, scalar2=None, op0=ALU.mult))(),
                r=[p_], w=[mean])
        S.add("pool", lambda e: e.tensor_tensor(out=o0[:], in0=o0[:], in1=mean[:], op=ALU.subtract), r=[o0, mean], w=[o0])
        S.add("act", lambda e: e.activation(out=sq[:], in_=o0[:], func=AF.Square), r=[o0], w=[sq])
        for nb in range(4):
            p_ = ps[pi % 8]
            pi += 1
            S.add("pe", (lambda p_=p_, nb=nb: lambda e: e.matmul(p_[:, :], lhsT=K["bones"][:], rhs=sq[:, nb * 512:(nb + 1) * 512],
                                                               start=True, stop=True))(), r=[sq, K["bones"]], w=[p_])
            S.add("dve", (lambda p_=p_, nb=nb: lambda e: e.tensor_scalar(
                out=rstd[:, nb * 512:(nb + 1) * 512], in0=p_[:, :], scalar1=1.0 / 64, scalar2=64e-5,
                op0=ALU.mult, op1=ALU.add))(), r=[p_], w=[rstd])
        S.add("act", lambda e: e.activation(out=rstd[:], in_=rstd[:], func=AF.Sqrt), r=[rstd], w=[rstd])
        S.add("dve", lambda e: e.reciprocal(out=rstd[:], in_=rstd[:]), r=[rstd], w=[rstd])
        S.add("pool", lambda e: e.tensor_tensor(out=o0[:], in0=o0[:], in1=rstd[:], op=ALU.mult), r=[o0, rstd], w=[o0])
        S.add("dve", (lambda c=c: lambda e: e.tensor_scalar(out=o0[:], in0=o0[:], scalar1=col("lnx_w", c),
                                                          scalar2=col("lnx_b", c), op0=ALU.mult, op1=ALU.add))(),
              r=[o0, K["cols"]], w=[o0])
        S.add("pool", lambda e: e.tensor_tensor(out=o0[:], in0=o0[:], in1=bn[:], op=ALU.add), r=[o0, bn], w=[o0])
        x_ = xo[c % 2]
        S.add("dve", (lambda x_=x_: lambda e: e.tensor_tensor(out=x_[:], in0=o0[:], in1=gt[:], op=ALU.mult))(),
              r=[o0, gt], w=[x_])
        S.dma("pool", XO[rows, :], x_[:], r=[x_])
    C.close()


def epi_relu2(S, C, dst, NT):
    t32 = [C.sb([128, NT], F32, "r2") for _ in range(2)]
    ot = [C.sb([128, NT], BF16, "r2o") for _ in range(2)]

    def epi(tb, m, acc, it):
        t_, o_ = t32[it % 2], ot[it % 2]
        S.add("act", lambda e: e.activation(out=t_[:], in_=acc[:], func=AF.Relu), r=[acc], w=[t_])
        S.add("dve", lambda e: e.tensor_tensor(out=o_[:], in0=t_[:], in1=t_[:], op=ALU.mult), r=[t_], w=[o_])
        S.dma("pool", dst[m * 128:(m + 1) * 128, :], o_[:], r=[o_])
    return epi


def phase_mlp(nc, S, K, H, w_up, w_down, HID, YFM):
    C = Ctx(nc, S)
    gemm_fm(nc, S, C, H, D, T, w_up, DFF, epi_relu2(S, C, HID, T))
    C.close()
    C = Ctx(nc, S)
    gemm_fm(nc, S, C, HID, DFF, T, w_down, D, epi_store(S, YFM, TB=512), TB=512)
    C.close()


def phase_hy_in(nc, S, K, H, hy_in_w, TM3):
    C = Ctx(nc, S)
    zt = [C.sb([128, T + 2], F32, "zt") for _ in range(2)]
    o32 = [C.sb([128, T], F32, "o32") for _ in range(2)]
    ob = [C.sb([128, T], BF16, "ob") for _ in range(2)]
    oT = [C.sb([128, 16, 128], BF16, "oT") for _ in range(2)]
    pH = C.ps([128, 2048], BF16)
    col = lambda name, c: K["cols"][:, CI[name], c:c + 1]
    for z_ in zt:
        S.add("pool", (lambda z_=z_: lambda e: e.memset(z_[:], 0.0))(), w=[z_])

    def epi(tb, m, acc, it):
        s_, c = m // 16, m % 16
        z_, o_, b_, t_ = zt[it % 2], o32[it % 2], ob[it % 2], oT[it % 2]
        S.add("act", lambda e: e.activation(out=z_[:, 1:T + 1], in_=acc[:], func=AF.Identity, bias=col("hy_in_b%d" % s_, c)),
              r=[acc, K["cols"]], w=[z_])
        S.add("dve", lambda e: e.tensor_scalar(out=o_[:], in0=z_[:, 0:T], scalar1=col("cw0_%d" % s_, c),
                                               scalar2=col("cb_%d" % s_, c), op0=ALU.mult, op1=ALU.add), r=[z_, K["cols"]], w=[o_])
        S.add("dve", lambda e: e.scalar_tensor_tensor(out=o_[:], in0=z_[:, 1:T + 1], scalar=col("cw1_%d" % s_, c), in1=o_[:],
                                                      op0=ALU.mult, op1=ALU.add), r=[z_, o_, K["cols"]], w=[o_])
        S.add("dve", lambda e: e.scalar_tensor_tensor(out=b_[:], in0=z_[:, 2:T + 2], scalar=col("cw2_%d" % s_, c), in1=o_[:],
                                                      op0=ALU.mult, op1=ALU.add), r=[z_, o_, K["cols"]], w=[b_])
        for tc in range(16):
            S.add("pe", (lambda tc=tc: lambda e: e.transpose(out=pH[:, tc * 128:(tc + 1) * 128], in_=b_[:, tc * 128:(tc + 1) * 128],
                                                             identity=K["identB"][:]))(), r=[b_, K["identB"]], w=[pH])
        S.add("act", lambda e: e.activation(out=t_[:, 0:8, :], in_=pH[:, 0:1024], func=AF.Copy), r=[pH], w=[t_])
        S.add("dve", lambda e: e.tensor_copy(out=t_[:, 8:16, :], in_=pH[:, 1024:2048]), r=[pH], w=[t_])
        S.dma("pool", TM3[s_].rearrange("(tc p) d -> p tc d", p=128)[:, :, c * 128:(c + 1) * 128], t_[:], r=[t_])

    gemm_fm(nc, S, C, H, D, T, hy_in_w, 3 * D, epi, nps=6)
    C.close()


PI = 3.141592
TWO_PI = 6.283185307179586


def phase_hy_filt(nc, S, K, HC, HS, HD):
    C = Ctx(nc, S)
    z0 = C.sb([33, T])
    w1 = C.sb([33, 64])
    w23 = C.sb([64, 2, 64])
    w4 = C.sb([64, 4 * D])
    fbq = C.sb([64, 6])
    za, zb, msk = C.sb([64, T]), C.sb([64, T]), C.sb([64, T])
    delta = C.sb([128, D])
    negt = C.sb([128, 16])
    win = C.sb([128, D])
    fr = [C.sb([128, D]) for _ in range(4)]
    osd = [C.sb([128, D], BF16) for _ in range(4)]
    ps = [C.ps() for _ in range(8)]
    S.dma("sp", z0[:], HC["z0T"], w=[z0])
    S.dma("sp", w1[:], HC["f_w1"], w=[w1])
    S.dma("sp", w23[:], HC["f_w23"].rearrange("m k n -> k m n"), w=[w23])
    S.dma("sp", w4[:], HC["f_w4"], w=[w4])
    S.dma("sp", fbq[:], HC["fbq"], w=[fbq])
    S.dma("sp", delta[:], HC["delta"].partition_broadcast(128), w=[delta])
    S.dma("sp", negt[:], HC["negt"], w=[negt])
    cur = z0
    kdim = 33
    for layer in range(3):
        wl = w1[:, :] if layer == 0 else w23[:, layer - 1, :]
        wres = w1 if layer == 0 else w23
        nxt = za if layer % 2 == 0 else zb
        for nb in range(4):
            p_ = ps[nb]
            S.add("pe", (lambda p_=p_, nb=nb, wl=wl, cur=cur, kdim=kdim: lambda e: e.matmul(
                p_[0:64, :], lhsT=wl, rhs=cur[0:kdim, nb * 512:(nb + 1) * 512], start=True, stop=True))(),
                r=[wres, cur], w=[p_])
            S.add("dve", (lambda p_=p_, nb=nb, nxt=nxt, layer=layer: lambda e: e.tensor_scalar(
                out=nxt[:, nb * 512:(nb + 1) * 512], in0=p_[0:64, :], scalar1=fbq[:, layer:layer + 1],
                scalar2=fbq[:, 3 + layer:4 + layer], op0=ALU.add, op1=ALU.mult))(), r=[p_, fbq], w=[nxt])
        S.add("dve", (lambda nxt=nxt: lambda e: e.tensor_scalar(out=msk[:], in0=nxt[:], scalar1=PI, scalar2=-TWO_PI,
                                                               op0=ALU.is_gt, op1=ALU.mult))(), r=[nxt], w=[msk])
        S.add("dve", (lambda nxt=nxt: lambda e: e.tensor_tensor(out=nxt[:], in0=nxt[:], in1=msk[:], op=ALU.add))(),
              r=[nxt, msk], w=[nxt])
        S.add("dve", (lambda nxt=nxt: lambda e: e.tensor_scalar(out=msk[:], in0=nxt[:], scalar1=-PI, scalar2=TWO_PI,
                                                               op0=ALU.is_lt, op1=ALU.mult))(), r=[nxt], w=[msk])
        S.add("dve", (lambda nxt=nxt: lambda e: e.tensor_tensor(out=nxt[:], in0=nxt[:], in1=msk[:], op=ALU.add))(),
              r=[nxt, msk], w=[nxt])
        S.add("dve", (lambda nxt=nxt: lambda e: e.tensor_scalar(out=nxt[:], in0=nxt[:], scalar1=-PI, scalar2=PI,
                                                               op0=ALU.max, op1=ALU.min))(), r=[nxt], w=[nxt])
        S.add("act", (lambda nxt=nxt: lambda e: e.activation(out=nxt[:], in_=nxt[:], func=AF.Sin))(), r=[nxt], w=[nxt])
        cur = nxt
        kdim = 64
    z3 = cur
    pi = 0
    for lc in range(16):
        S.add("act", (lambda lc=lc: lambda e: e.activation(out=win[:], in_=delta[:], func=AF.Exp, scale=negt[:, lc:lc + 1]))(),
              r=[delta, negt], w=[win])
        for q in range(4):
            f_ = fr[q]
            for nb in range(4):
                p_ = ps[pi % 8]
                pi += 1
                S.add("pe", (lambda p_=p_, q=q, nb=nb, lc=lc: lambda e: e.matmul(
                    p_[:, :], lhsT=z3[:, lc * 128:(lc + 1) * 128], rhs=w4[:, q * D + nb * 512:q * D + (nb + 1) * 512],
                    start=True, stop=True))(), r=[z3, w4], w=[p_])
                S.add("dve", (lambda p_=p_, f_=f_, nb=nb: lambda e: e.tensor_tensor(
                    out=f_[:, nb * 512:(nb + 1) * 512], in0=p_[:, :], in1=win[:, nb * 512:(nb + 1) * 512], op=ALU.mult))(),
                    r=[p_, win], w=[f_])
        for o in range(2):
            hf, hb = fr[o], fr[2 + o]
            if lc == 0:
                S.add("pool", (lambda hb=hb: lambda e: e.memset(hb[0:1, :], 0.0))(), w=[hb])
            s_, d_ = osd[o], osd[2 + o]
            S.add("pool", (lambda hf=hf, hb=hb, s_=s_: lambda e: e.tensor_tensor(out=s_[:], in0=hf[:], in1=hb[:], op=ALU.add))(),
                  r=[hf, hb], w=[s_])
            S.add("pool", (lambda hf=hf, hb=hb, d_=d_: lambda e: e.tensor_tensor(out=d_[:], in0=hf[:], in1=hb[:], op=ALU.subtract))(),
                  r=[hf, hb], w=[d_])
            S.dma("sp", HS[o][lc * 128:(lc + 1) * 128, :], s_[:], r=[s_])
            S.dma("act", HD[o][lc * 128:(lc + 1) * 128, :], d_[:], r=[d_])
    C.close()


def dft_pass(nc, S, C, Xdram, FTp, rcs, epi, nyq=None):
    Xs = C.sb([128, 16, D], BF16, "dX")
    wt = [C.sb([128, 16, 128], BF16, "dW") for _ in range(2)]
    ps = [C.ps() for _ in range(8)]
    xv = Xdram.rearrange("(tc p) d -> p tc d", p=128)
    S.dma("sp", Xs[:, 0:8, :], xv[:, 0:8, :], w=[Xs])
    S.dma("act", Xs[:, 8:16, :], xv[:, 8:16, :], w=[Xs])
    pi = 0
    for it, rc in enumerate(rcs):
        w_ = wt[it % 2]
        S.dma("sp", w_[:], FTp[rc], w=[w_])
        pb = []
        for nb in range(4):
            p_ = ps[pi % 8]
            pi += 1
            pb.append(p_)
            for tc in range(16):
                S.add("pe", (lambda p_=p_, w_=w_, tc=tc, nb=nb: lambda e: e.matmul(
                    p_[:, :], lhsT=w_[:, tc, :], rhs=Xs[:, tc, nb * 512:(nb + 1) * 512], start=(tc == 0), stop=(tc == 15)))(),
                    r=[w_, Xs], w=[p_])
        epi(it, rc, pb)
    if nyq is not None:
        w_ = wt[len(rcs) % 2]
        S.dma("sp", w_[:], FTp[16], w=[w_])
        pb = []
        for nb in range(4):
            p_ = ps[pi % 8]
            pi += 1
            pb.append(p_)
            for tc in range(16):
                S.add("pe", (lambda p_=p_, w_=w_, tc=tc, nb=nb: lambda e: e.matmul(
                    p_[0:1, :], lhsT=w_[:, tc, 0:1], rhs=Xs[:, tc, nb * 512:(nb + 1) * 512], start=(tc == 0), stop=(tc == 15)))(),
                    r=[w_, Xs], w=[p_])
        nyq(pb)


def phase_hy_spec(nc, S, K, HS, HD, FTp, KH):
    for o in range(2):
        for (X, rcs, isre) in ((HD[o], list(range(16, 32)), False), (HS[o], list(range(0, 16)), True)):
            C = Ctx(nc, S)
            ot = [C.sb([128, D], F32, "ko") for _ in range(2)]
            nrow = C.sb([1, D], F32, "nr")

            def epi(it, rc, pb, o=o, ot=ot):
                o_ = ot[it % 2]
                for nb in range(4):
                    if nb % 2:
                        S.add("act", (lambda nb=nb: lambda e: e.activation(out=o_[:, nb * 512:(nb + 1) * 512], in_=pb[nb][:, :],
                                                                           func=AF.Copy))(), r=[pb[nb]], w=[o_])
                    else:
                        S.add("dve", (lambda nb=nb: lambda e: e.tensor_copy(out=o_[:, nb * 512:(nb + 1) * 512], in_=pb[nb][:, :]))(),
                              r=[pb[nb]], w=[o_])
                S.dma("pool", KH[o][rc * 128:(rc + 1) * 128, :], o_[:], r=[o_])

            def nyq(pb, o=o, nrow=nrow):
                for nb in range(4):
                    S.add("dve", (lambda nb=nb: lambda e: e.tensor_copy(out=nrow[0:1, nb * 512:(nb + 1) * 512], in_=pb[nb][0:1, :]))(),
                          r=[pb[nb]], w=[nrow])
                S.dma("pool", KH[o][2048:2049, :], nrow[:], r=[nrow])

            dft_pass(nc, S, C, X, FTp, rcs, epi, nyq if isre else None)
            C.close()


def phase_hy_conv(nc, S, K, o, Xin, KH, FTp, Gp, YH, gate, skip_row, Yout_tm, Yout_fm):
    C = Ctx(nc, S)
    vr, vi = C.sb([128, D]), C.sb([128, D])
    kr, ki = C.sb([128, D]), C.sb([128, D])
    t1, t2 = C.sb([128, D]), C.sb([128, D])
    yr = [C.sb([128, D], BF16) for _ in range(2)]
    yi = [C.sb([128, D], BF16) for _ in range(2)]
    rcs = []
    for j in range(16):
        rcs += [j, 16 + j]

    def epi(it, rc, pb):
        j = rc % 16
        dst = vr if rc < 16 else vi
        kd_ = kr if rc < 16 else ki
        S.dma("act", kd_[:], KH[rc * 128:(rc + 1) * 128, :], w=[kd_])
        for nb in range(4):
            S.add("act", (lambda nb=nb: lambda e: e.activation(out=dst[:, nb * 512:(nb + 1) * 512], in_=pb[nb][:, :],
                                                               func=AF.Copy))(), r=[pb[nb]], w=[dst])
        if rc >= 16:
            yr_, yi_ = yr[j % 2], yi[j % 2]
            S.add("dve", lambda e: e.tensor_tensor(out=t1[:], in0=vr[:], in1=kr[:], op=ALU.mult), r=[vr, kr], w=[t1])
            S.add("pool", lambda e: e.tensor_tensor(out=t2[:], in0=vi[:], in1=ki[:], op=ALU.mult), r=[vi, ki], w=[t2])
            S.add("dve", lambda e: e.tensor_tensor(out=yr_[:], in0=t1[:], in1=t2[:], op=ALU.subtract), r=[t1, t2], w=[yr_])
            S.add("pool", lambda e: e.tensor_tensor(out=t1[:], in0=vr[:], in1=ki[:], op=ALU.mult), r=[vr, ki], w=[t1])
            S.add("dve", lambda e: e.tensor_tensor(out=t2[:], in0=vi[:], in1=kr[:], op=ALU.mult), r=[vi, kr], w=[t2])
            S.add("pool", lambda e: e.tensor_tensor(out=yi_[:], in0=t1[:], in1=t2[:], op=ALU.add), r=[t1, t2], w=[yi_])
            if j == 0:
                S.add("dve", lambda e: e.tensor_tensor(out=yr_[0:1, :], in0=vr[0:1, :], in1=kr[0:1, :], op=ALU.mult),
                      r=[vr, kr, yr_], w=[yr_])
                S.add("dve", lambda e: e.tensor_tensor(out=yi_[0:1, :], in0=vi[0:1, :], in1=ki[0:1, :], op=ALU.mult),
                      r=[vi, ki, yi_], w=[yi_])
            S.dma("pool", YH[j * 128:(j + 1) * 128, :], yr_[:], r=[yr_])
            S.dma("pool", YH[(16 + j) * 128:(17 + j) * 128, :], yi_[:], r=[yi_])

    dft_pass(nc, S, C, Xin, FTp, rcs, epi)
    C.close()
    C = Ctx(nc, S)
    HW = D // 2
    Ys = C.sb([128, 32, HW], BF16, "iY")
    gw = [C.sb([128, 32, 128], BF16, "iW") for _ in range(2)]
    gt = [C.sb([128, HW], BF16) for _ in range(2)]
    yp = [C.sb([128, HW], BF16) for _ in range(2)]
    tt = [C.sb([128, HW]) for _ in range(2)]
    yo = [C.sb([128, HW], BF16) for _ in range(2)]
    skb = C.sb([128, D])
    S.dma("sp", skb[:], skip_row.partition_broadcast(128), w=[skb])
    ps = [C.ps() for _ in range(4)]
    if Yout_fm is not None:
        pH = C.ps([128, 1024], BF16)
        oT = [C.sb([128, 8, 128], BF16) for _ in range(2)]
        yfv = Yout_fm.rearrange("(c p) t -> p c t", p=128)
    yv = YH.rearrange("(rc p) d -> p rc d", p=128)
    it = 0
    pi = 0
    for hf in range(2):
        cs_ = slice(hf * HW, (hf + 1) * HW)
        S.dma("sp", Ys[:, 0:16, :], yv[:, 0:16, cs_], w=[Ys])
        S.dma("act", Ys[:, 16:32, :], yv[:, 16:32, cs_], w=[Ys])
        for tcn in range(16):
            w_, g_, p__, t_, o_ = gw[it % 2], gt[it % 2], yp[it % 2], tt[it % 2], yo[it % 2]
            S.dma("sp", w_[:], Gp[tcn], w=[w_])
            S.dma("act", g_[:], gate[tcn * 128:(tcn + 1) * 128, cs_], w=[g_])
            S.dma("act", p__[:], Xin[tcn * 128:(tcn + 1) * 128, cs_], w=[p__])
            S.add("pool", (lambda t_=t_, p__=p__, cs_=cs_: lambda e: e.tensor_tensor(out=t_[:], in0=p__[:], in1=skb[:, cs_], op=ALU.mult))(),
                  r=[p__, skb], w=[t_])
            for nb in range(2):
                p_ = ps[pi % 4]
                pi += 1
                for rc in range(32):
                    S.add("pe", (lambda p_=p_, w_=w_, rc=rc, nb=nb: lambda e: e.matmul(
                        p_[:, :], lhsT=w_[:, rc, :], rhs=Ys[:, rc, nb * 512:(nb + 1) * 512], start=(rc == 0), stop=(rc == 31)))(),
                        r=[w_, Ys], w=[p_])
                S.add("dve", (lambda p_=p_, t_=t_, nb=nb: lambda e: e.tensor_tensor(
                    out=t_[:, nb * 512:(nb + 1) * 512], in0=p_[:, :], in1=t_[:, nb * 512:(nb + 1) * 512], op=ALU.add))(),
                    r=[p_, t_], w=[t_])
            S.add("pool", (lambda t_=t_, g_=g_, o_=o_: lambda e: e.tensor_tensor(out=o_[:], in0=t_[:], in1=g_[:], op=ALU.mult))(),
                  r=[t_, g_], w=[o_])
            if Yout_tm is not None:
                S.dma("pool", Yout_tm[tcn * 128:(tcn + 1) * 128, cs_], o_[:], r=[o_])
            if Yout_fm is not None:
                oT_ = oT[it % 2]
                for c in range(8):
                    S.add("pe", (lambda c=c, o_=o_: lambda e: e.transpose(out=pH[:, c * 128:(c + 1) * 128], in_=o_[:, c * 128:(c + 1) * 128],
                                                                          identity=K["identB"][:]))(), r=[o_, K["identB"]], w=[pH])
                S.add("act", (lambda oT_=oT_: lambda e: e.activation(out=oT_[:], in_=pH[:, :], func=AF.Copy))(), r=[pH], w=[oT_])
                S.dma("pool", yfv[:, hf * 8:(hf + 1) * 8, tcn * 128:(tcn + 1) * 128], oT_[:], r=[oT_])
            it += 1
    C.close()


def build(dbg=(), upto=99, nsteps=18, lvl=9):
    nc = bass.Bass("TRN2", target_bir_lowering=False)

    def din(name, shape, dt=F32):
        return nc.dram_tensor(name, list(shape), dt, kind="ExternalInput").ap()

    def dint(name, shape, dt=F32):
        kind = "ExternalOutput" if name in dbg else "Internal"
        return nc.dram_tensor(name, list(shape), dt, kind=kind).ap()

    cc = din("cc", [128, 16, 2])
    ada_w = din("ada_w", [2, D, 6 * D])
    ada_b = din("ada_b", [2, 6 * D])
    x = din("x", [T, D])
    ctx = din("ctx", [TC, D])
    norm_g = din("norm_g", [2, 4, D])
    identF = din("identF", [128, 128])
    identB = din("identB", [128, 128], BF16)
    bones = din("bones", [128, 128])
    cols = din("cols", [128, len(COLS), 16])
    mu = din("mu", [128, 16, 6])
    P = {}
    for n, shp in (("rw_w_r", [16, 128, 16, 128]), ("rw_w_k", [16, 128, 16, 128]), ("rw_w_v", [16, 128, 16, 128]),
                   ("rw_w_o", [16, 128, 16, 128]),
                   ("rw_dec_w1", [2, 1, 128, 16, 96]), ("rw_dec_w2", [2, 96, D]), ("rw_a1", [2, 1, 128, 16, 96]), ("rw_a2", [2, 96, D]),
                   ("rw_g1", [2, 128, 16, 128]), ("rw_g2", [16, 128, 2, 128])):
        P[n] = din(n, shp)
    modv = dint("modv", [2, 2, 6 * D])
    h_fm = dint("h_fm", [D, TT], BF16)
    xm = [dint("xm%d" % j, [D, TT], BF16) for j in range(6)]
    Dr = {"r": dint("r_fm", [D, TT]), "k": dint("k_fm", [D, TT]), "v": dint("v_fm", [D, TT]),
          "t1": dint("t1", [96, TT], BF16), "t2": dint("t2", [256, TT], BF16),
          "logw": [dint("logw%d" % e, [D, TT]) for e in range(2)],
          "a": [dint("a%d" % e, [D, TT]) for e in range(2)], "g": dint("g_fm", [D, TT]),
          "kkn": dint("kkn", [D, TT]), "kd": [dint("kd%d" % e, [D, TT]) for e in range(2)],
          "b": [dint("b%d" % e, [D, TT]) for e in range(2)], "bonus": dint("bonus", [D, TT])}
    Odram = [dint("o%d" % e, [D, T]) for e in range(2)]
    scan_consts = {"mask01": din("mask01", [128, 16, 128]), "mask": [din("mask_%d" % e, [128, 256]) for e in range(2)],
                   "maskT": [din("maskT_%d" % e, [128, 128]) for e in range(2)], "id2": din("id2", [128, 64]),
                   "md": [din("md_%d" % e, [128, 128]) for e in range(2)], "m1c": [din("m1c_%d" % e, [128, 128]) for e in range(2)],
                   "m2c": [din("m2c_%d" % e, [128, 128]) for e in range(2)], "mtd": [din("mtd_%d" % e, [128, 128]) for e in range(2)]}
    XO = dint("XO", [D, T], BF16)
    YFM = dint("YFM", [D, T])
    XA, XB, XC = dint("XA", [T, D]), dint("XB", [T, D]), dint("XC", [T, D])
    H2, H1 = dint("H2", [D, T], BF16), dint("H1", [D, T], BF16)
    HID = dint("HID", [DFF, T], BF16)
    mlp_up, mlp_down = din("mlp_up", [2, 64, 128, 16, 128]), din("mlp_down", [2, 16, 128, 64, 128])
    hy_in_w, hy_out_w = din("hy_in_w", [48, 128, 16, 128]), din("hy_out_w", [16, 128, 16, 128])
    hy_skip = din("hy_skip", [2, D])
    HC = {"z0T": din("z0T", [33, T]), "f_w1": din("f_w1", [33, 64]), "f_w23": din("f_w23", [2, 64, 64]),
          "f_w4": din("f_w4", [64, 4 * D]), "fbq": din("fbq", [64, 6]), "delta": din("delta", [D]), "negt": din("negt", [128, 16])}
    FTp = din("FTp", [32, 128, 16, 128], BF16)
    Gp = din("Gp", [16, 128, 32, 128], BF16)
    TM3 = [dint("TM3_%d" % i, [T, D], BF16) for i in range(3)]
    HS = [dint("HS%d" % o, [T, D], BF16) for o in range(2)]
    HD = [dint("HD%d" % o, [T, D], BF16) for o in range(2)]
    KH = [dint("KH%d" % o, [2 * T, D]) for o in range(2)]
    YH = dint("YH", [2 * T, D], BF16)
    Y1T = dint("Y1T", [T, D], BF16)
    Y2FM = dint("Y2FM", [D, T], BF16)
    outp = nc.dram_tensor("out", [T, D], F32, kind="ExternalOutput").ap()
    mrow = lambda l, row, j: modv[l, row, j * D:(j + 1) * D]
    with ExitStack() as st:
        S = Sched(nc, st)
        phase_clear(nc, S)
        K = {}
        for n, shp, dt in (("identF", [128, 128], F32), ("identB", [128, 128], BF16), ("bones", [128, 128], F32),
                           ("cols", [128, len(COLS), 16], F32), ("mu", [128, 16, 6], F32), ("omu", [128, 16, 6], F32)):
            K[n] = st.enter_context(nc.sbuf_tensor("K_" + n, shp, dt))
        phase_consts(nc, S, K, identF, identB, bones, cols, mu)
        phase_ada(nc, S, cc, ada_w, ada_b, modv)
        if upto >= 1:
            mrow = lambda l, row, j: modv[l, row, j * D:(j + 1) * D]
            phase_norm(nc, S, K, ctx, TC, None, None, h_fm, (0, TC + T), None,
                       (norm_g[0, 0], mrow(0, 1, 1), mrow(0, 1, 0)))
            phase_norm(nc, S, K, x, T, None, None, h_fm, (TC,), None,
                       (norm_g[0, 0], mrow(0, 0, 1), mrow(0, 0, 0)))
        if upto >= 2:
            phase_mix(nc, S, K, h_fm, xm)
        if upto >= 3:
            phase_rwkv_proj(nc, S, K, xm, P, Dr)
        if upto >= 4:
            phase_scanprep(nc, S, K, Dr)
        if upto >= 5:
            phase_scan(nc, S, K, Dr, Odram, scan_consts, TDT=F32, nsteps=nsteps, lvl=lvl)
        if upto >= 6:
            colf = lambda name: (lambda m: K["cols"][:, CI[name], m:m + 1])
            phase_readout(nc, S, K, Dr, Odram, XO)
            C = Ctx(nc, S)
            gemm_fm(nc, S, C, XO, D, T, P["rw_w_o"], D, epi_store(S, YFM))
            C.close()
            phase_norm(nc, S, K, x, T, YFM, XA, H2, (0,), (mrow(0, 0, 2), norm_g[0, 1]),
                       (norm_g[0, 2], mrow(0, 0, 4), mrow(0, 0, 3)))
            phase_mlp(nc, S, K, H2, mlp_up[0], mlp_down[0], HID, YFM)
            phase_norm(nc, S, K, XA, T, YFM, XB, H1, (0,), (mrow(0, 0, 5), norm_g[0, 3]),
                       (norm_g[1, 0], mrow(1, 0, 1), mrow(1, 0, 0)))
        if upto >= 7:
            phase_hy_in(nc, S, K, H1, hy_in_w, TM3)
            phase_hy_filt(nc, S, K, HC, HS, HD)
            phase_hy_spec(nc, S, K, HS, HD, FTp, KH)
            phase_hy_conv(nc, S, K, 0, TM3[0], KH[0], FTp, Gp, YH, TM3[1], hy_skip[0], Y1T, None)
            phase_hy_conv(nc, S, K, 1, Y1T, KH[1], FTp, Gp, YH, TM3[2], hy_skip[1], None, Y2FM)
            C = Ctx(nc, S)
            gemm_fm(nc, S, C, Y2FM, D, T, hy_out_w, D, epi_act(S, C, YFM, AF.Identity, T, odt=F32, bias=colf("hy_out_b")))
            C.close()
            phase_norm(nc, S, K, XB, T, YFM, XC, H2, (0,), (mrow(1, 0, 2), norm_g[1, 1]),
                       (norm_g[1, 2], mrow(1, 0, 4), mrow(1, 0, 3)))
            phase_mlp(nc, S, K, H2, mlp_up[1], mlp_down[1], HID, YFM)
            phase_norm(nc, S, K, XC, T, YFM, outp, None, (), (mrow(1, 0, 5), norm_g[1, 3]), None)
    return nc


def wtile(w, msz=128):
    w = np.asarray(w, np.float32)
    Kd, N = w.shape
    return np.ascontiguousarray(w.reshape(Kd // 128, 128, N // msz, msz).transpose(2, 1, 0, 3))


def tiled_weights(inputs):
    f = lambda a: np.ascontiguousarray(np.asarray(a, dtype=np.float32))
    m = {}
    for n in ("rw_w_r", "rw_w_k", "rw_w_v", "rw_w_o", "rw_g1", "rw_g2"):
        m[n] = wtile(inputs[n][0])
    for n in ("rw_dec_w1", "rw_a1"):
        m[n] = np.stack([wtile(inputs[n][0][e], 96) for e in range(2)], axis=0)
    for n in ("rw_dec_w2", "rw_a2"):
        m[n] = f(inputs[n][0])
    m["mlp_up"] = np.stack([wtile(inputs["mlp_up"][l]) for l in range(2)], axis=0)
    m["mlp_down"] = np.stack([wtile(inputs["mlp_down"][l]) for l in range(2)], axis=0)
    m["hy_in_w"] = wtile(inputs["hy_in_w"][0])
    m["hy_out_w"] = wtile(inputs["hy_out_w"][0])
    return m


def colpack(v):
    return np.asarray(v, np.float32).reshape(16, 128).T


def make_inputs(inputs, b, tw):
    f = lambda a: np.ascontiguousarray(np.asarray(a, dtype=np.float32))
    m = {}
    cc = np.zeros((128, 16, 2), np.float32)
    cc[:, :, 0] = colpack(inputs["c"][b])
    cc[:, :, 1] = colpack(inputs["c_ctx"])
    m["cc"] = cc
    m["ada_w"] = f(inputs["ada_w"])
    m["ada_b"] = f(inputs["ada_b"])
    m["x"] = f(inputs["x"][b])
    m["ctx"] = f(inputs["ctx"][b])
    m["norm_g"] = f(inputs["norm_g"])
    m["identF"] = np.eye(128, dtype=np.float32)
    m["identB"] = np.eye(128).astype(ml_dtypes.bfloat16)
    bo = np.zeros((128, 128), np.float32)
    bo[:64, :64] = 1.0
    bo[64:, 64:] = 1.0
    m["bones"] = bo
    cv = {"k_k": inputs["rw_k_k"][0], "k_a": inputs["rw_k_a"][0], "r_k": np.asarray(inputs["rw_r_k"][0]).reshape(-1),
          "lnx_w": inputs["rw_lnx_w"][0], "lnx_b": inputs["rw_lnx_b"][0],
          "dec_w0_0": inputs["rw_dec_w0"][0, 0], "dec_w0_1": inputs["rw_dec_w0"][0, 1],
          "a0_0": inputs["rw_a0"][0, 0], "a0_1": inputs["rw_a0"][0, 1], "hy_out_b": inputs["hy_out_b"][0]}
    for s in range(3):
        cv["hy_in_b%d" % s] = inputs["hy_in_b"][0, s * D:(s + 1) * D]
        cv["cb_%d" % s] = inputs["hy_conv_b"][0, s * D:(s + 1) * D]
        for tap in range(3):
            cv["cw%d_%d" % (tap, s)] = inputs["hy_conv_w"][0, tap, s * D:(s + 1) * D]
    colsarr = np.zeros((128, len(COLS), 16), np.float32)
    for n in COLS:
        colsarr[:, CI[n], :] = colpack(cv[n])
    m["cols"] = colsarr
    mu = np.asarray(inputs["rw_mu"][0], np.float32)
    m["mu"] = np.ascontiguousarray(mu.reshape(6, 16, 128).transpose(2, 1, 0))
    m01 = np.ones((128, 16, 128), np.float32)
    m01[:, :, 0] = 0.0
    m["mask01"] = m01
    ii = np.arange(128)[:, None]
    tt = np.arange(128)[None, :]
    m["mask_0"] = np.concatenate([(ii < tt), (ii <= tt)], axis=1).astype(np.float32)
    m["mask_1"] = np.concatenate([(ii > tt), (ii >= tt)], axis=1).astype(np.float32)
    m["maskT_0"] = (ii > tt).astype(np.float32)
    m["maskT_1"] = (ii < tt).astype(np.float32)
    b32 = (ii // 32) == (tt // 32)
    b64 = (ii // 64) == (tt // 64)
    for e_, st_ in ((0, ii < tt), (1, ii > tt)):
        m["md_%d" % e_] = (st_ & b32).astype(np.float32)
        m["m1c_%d" % e_] = (st_ & b64 & ~b32).astype(np.float32)
        m["m2c_%d" % e_] = (st_ & ~b64).astype(np.float32)
        m["mtd_%d" % e_] = (st_.T & b32).astype(np.float32)
    id2 = np.zeros((128, 64), np.float32)
    id2[np.arange(128), np.arange(128) % 64] = 1.0
    m["id2"] = id2
    m.update(tw)
    m["hy_skip"] = f(inputs["hy_skip"][0])
    m["f_w1"] = f(inputs["hy_f_w1"][0])
    m["f_w23"] = f(inputs["hy_f_w23"][0])
    m["f_w4"] = f(inputs["hy_f_w4"][0])
    m["fbq"] = np.ascontiguousarray(np.concatenate([np.asarray(inputs["hy_f_b"][0], np.float32).T,
                                                    np.asarray(inputs["hy_f_freq"][0], np.float32).T], axis=1))
    m.update(HCONST)
    return m


def _hconst():
    L_ = T
    t = np.linspace(0.0, 1.0, L_, dtype=np.float32)
    freqs = np.linspace(1e-4, 15.0, 16, dtype=np.float32)[None, :]
    ang = (np.float32(2.0 * np.pi / L_) * np.arange(L_, dtype=np.float32)[:, None]) * freqs
    z0 = np.concatenate([t[:, None], np.cos(ang), -np.sin(ang)], axis=-1).astype(np.float32)
    c = {"z0T": np.ascontiguousarray(z0.T)}
    max_decay = np.log(1e-2) / 0.3
    min_decay = np.log(1e-2) / 1.5
    c["delta"] = np.abs(np.linspace(min_decay, max_decay, D, dtype=np.float32)).astype(np.float32)
    c["negt"] = np.ascontiguousarray((-t).reshape(16, 128).T)
    tt = np.arange(T, dtype=np.float64)[:, None]
    ff = np.arange(2048, dtype=np.float64)[None, :]
    th = 2.0 * np.pi * ff * tt / 4096.0
    FT = np.concatenate([np.cos(th), -np.sin(th)], axis=1)
    FT[:, 2048] = (-1.0) ** np.arange(T)
    sc = np.full(4096, 2.0 / 4096.0)
    sc[0] = 1.0 / 4096.0
    sc[2048] = 1.0 / 4096.0
    G = (FT * sc[None, :]).T
    c["FTp"] = np.ascontiguousarray(FT.reshape(16, 128, 32, 128).transpose(2, 1, 0, 3)).astype(ml_dtypes.bfloat16)
    c["Gp"] = np.ascontiguousarray(G.reshape(32, 128, 16, 128).transpose(2, 1, 0, 3)).astype(ml_dtypes.bfloat16)
    return c


HCONST = _hconst()


def run(inputs, dbg=(), cores=NCORES, trace=False, upto=99, nsteps=18, lvl=9):
    nc = build(dbg, upto, nsteps, lvl)
    tw = tiled_weights(inputs)
    in_maps = [make_inputs(inputs, b, tw) for b in range(cores)]
    res = run_bass_kernel_spmd(nc, in_maps, core_ids=list(range(cores)), trace=trace)
    return res


def kernel(**inputs):
    res = run(inputs)
    out = np.stack([np.asarray(r["out"]) for r in res.results], axis=0)
    return out.astype(np.float32)
```

```python
import numpy as np
import ml_dtypes
from contextlib import ExitStack
import concourse.bass as bass
import concourse.mybir as mybir
from concourse.bass_utils import run_bass_kernel_spmd

F32 = mybir.dt.float32
BF16 = mybir.dt.bfloat16
AF = mybir.ActivationFunctionType
ALU = mybir.AluOpType
AX = mybir.AxisListType

T = 2048
D = 2048
TC = 256
TT = 2560
NH = 32
DFF = 8192
NCORES = 8


class Res:
    __slots__ = ("n", "excl")

    def __init__(self, n="", excl=False):
        self.n = n
        self.excl = excl


class Op:
    __slots__ = ("eng", "fn", "deps", "dma", "seq", "sem", "val")


class Sched:
    NSLOT = 8
    COMPUTE = ("pe", "dve", "act", "pool")
    QUEUES = ("sp", "pool", "act")

    def __init__(self, nc, stack):
        self.nc = nc
        self.sems = []
        self.esem = {}
        for e in self.COMPUTE:
            self.esem[e] = len(self.sems)
            self.sems.append(stack.enter_context(nc.semaphore("es_" + e)))
        self.dsem = {}
        for q in self.QUEUES:
            self.dsem[q] = []
            for i in range(self.NSLOT):
                self.dsem[q].append(len(self.sems))
                self.sems.append(stack.enter_context(nc.semaphore("ds_%s%d" % (q, i))))
        self.ecount = {e: 0 for e in self.COMPUTE}
        self.dcount = {q: 0 for q in self.QUEUES}
        self.known = {e: {} for e in ("pe", "dve", "act", "pool", "sp")}
        self.first = True
        self.reset_phase()

    def reset_phase(self):
        self.ops = []
        self.lastw = {}
        self.readers = {}

    def add(self, eng, fn, r=(), w=(), dma=False):
        ex = [k for k in r if getattr(k, "excl", False)]
        if ex:
            r = [k for k in r if not getattr(k, "excl", False)]
            w = list(w) + ex
        deps = set()
        for k in r:
            k = id(k)
            if k in self.lastw:
                deps.add(self.lastw[k])
        for k in w:
            k = id(k)
            if k in self.lastw:
                deps.add(self.lastw[k])
            deps.update(self.readers.get(k, ()))
        idx = len(self.ops)
        op = Op()
        op.eng = eng
        op.fn = fn
        op.deps = deps
        op.dma = dma
        if dma:
            k = self.dcount[eng]
            self.dcount[eng] += 1
            op.sem = self.dsem[eng][k % self.NSLOT]
            op.val = 16 * (k // self.NSLOT + 1)
            op.seq = None
        else:
            self.ecount[eng] += 1
            op.seq = self.ecount[eng]
            op.sem = self.esem[eng]
            op.val = op.seq
        self.ops.append(op)
        for k in r:
            self.readers.setdefault(id(k), []).append(idx)
        for k in w:
            self.lastw[id(k)] = idx
            self.readers[id(k)] = []
        return idx

    def dma(self, q, out, in_, r=(), w=()):
        return self.add(q, lambda e: e.dma_start(out=out, in_=in_), r, w, dma=True)

    def _wait(self, e, en, sem, val):
        if self.known[en].get(sem, 0) < val:
            e.wait_ge(self.sems[sem], val)
            self.known[en][sem] = val

    def _emit(self, e, en, op):
        waits = {}
        for d in op.deps:
            dop = self.ops[d]
            if (not dop.dma) and dop.eng == en and en == "pe":
                continue
            waits[dop.sem] = max(waits.get(dop.sem, 0), dop.val)
        if op.dma and op.val > 16:
            waits[op.sem] = max(waits.get(op.sem, 0), op.val - 16)
        for sem, val in waits.items():
            self._wait(e, en, sem, val)
        ins = op.fn(e)
        ins.then_inc(self.sems[op.sem], 16 if op.dma else 1)

    def flush(self):
        nc = self.nc
        ops = self.ops
        byeng = {}
        for i, op in enumerate(ops):
            byeng.setdefault(op.eng, []).append(i)
        first = self.first
        self.first = False
        with nc.Block() as block:
            deco = {"sp": block.sync, "dve": block.vector, "act": block.scalar,
                    "pool": block.gpsimd, "pe": block.tensor}

            def make(en):
                def body(e):
                    for i in byeng.get(en, []):
                        self._emit(e, en, ops[i])
                    if en == "sp":
                        for q in self.QUEUES:
                            k = self.dcount[q]
                            for s in range(self.NSLOT):
                                n = (k - s + self.NSLOT - 1) // self.NSLOT
                                if n > 0:
                                    self._wait(e, en, self.dsem[q][s], 16 * n)
                return body

            for en in ("sp", "dve", "act", "pool", "pe"):
                if en == "sp" or en in byeng:
                    deco[en](make(en))
        self.reset_phase()


class Ctx:
    n = 0

    def __init__(self, nc, S):
        self.nc = nc
        self.S = S
        self.st = ExitStack()

    def sb(self, shape, dt=F32, name=None):
        Ctx.n += 1
        t = self.st.enter_context(self.nc.sbuf_tensor("%s_%d" % (name or "t", Ctx.n), list(shape), dt))
        return t

    def ps(self, shape=(128, 512), dt=F32, name=None):
        Ctx.n += 1
        t = self.st.enter_context(self.nc.psum_tensor("%s_%d" % (name or "p", Ctx.n), list(shape), dt))
        return t

    def close(self):
        self.S.flush()
        self.st.close()


def ap(t):
    return t.ap() if hasattr(t, "ap") and callable(t.ap) else t


def phase_clear(nc, S):
    with nc.Block() as block:
        @block.sync
        def _(e):
            for s in S.sems:
                e.sem_clear(s)


def phase_ada(nc, S, cc, ada_w, ada_b, modout):
    C = Ctx(nc, S)
    s_raw = C.sb([128, 16, 2])
    s = C.sb([128, 16, 2])
    wt = [C.sb([128, 4096]) for _ in range(3)]
    ps = [C.ps() for _ in range(8)]
    acc = [C.sb([2, 4096]) for _ in range(2)]
    bt = [C.sb([2, 4096]) for _ in range(2)]
    S.dma("sp", s_raw[:], cc, w=[s_raw])
    S.add("act", lambda e: e.activation(out=s[:], in_=s_raw[:], func=AF.Silu), r=[s_raw], w=[s])
    cnt = 0
    gi = 0
    for l in range(2):
        for g in range(3):
            b_ = bt[gi % 2]
            a_ = acc[gi % 2]
            gi += 1
            S.dma("act", b_[:], ada_b[l, g * 4096:(g + 1) * 4096].partition_broadcast(2), w=[b_])
            for kc in range(16):
                w = wt[cnt % 3]
                cnt += 1
                S.dma("sp", w[:], ada_w[l, kc * 128:(kc + 1) * 128, g * 4096:(g + 1) * 4096], w=[w])
                for nb in range(8):
                    S.add("pe", (lambda nb=nb, kc=kc, w=w: lambda e: e.matmul(
                        ps[nb][0:2, :], lhsT=s[:, kc, :], rhs=w[:, nb * 512:(nb + 1) * 512],
                        start=(kc == 0), stop=(kc == 15)))(), r=[s, w], w=[ps[nb]])
            for nb in range(8):
                S.add("dve", (lambda nb=nb, a_=a_, b_=b_: lambda e: e.tensor_tensor(
                    out=a_[:, nb * 512:(nb + 1) * 512], in0=ps[nb][0:2, :],
                    in1=b_[:, nb * 512:(nb + 1) * 512], op=ALU.add))(), r=[ps[nb], b_], w=[a_])
            S.dma("pool", modout[l, :, g * 4096:(g + 1) * 4096], a_[:], r=[a_], w=[])
    C.close()


COLS = ["k_k", "k_a", "r_k", "lnx_w", "lnx_b", "dec_w0_0", "dec_w0_1", "a0_0", "a0_1",
        "hy_in_b0", "hy_in_b1", "hy_in_b2", "cw0_0", "cw0_1", "cw0_2", "cw1_0", "cw1_1", "cw1_2",
        "cw2_0", "cw2_1", "cw2_2", "cb_0", "cb_1", "cb_2", "hy_out_b"]
CI = {n: i for i, n in enumerate(COLS)}


def L(f):
    return f


def phase_norm(nc, S, K, x_in, ntok, y_fm, x_out, h_fm, hbases, gm_rows, h_rows):
    C = Ctx(nc, S)
    NB = 3
    xt = [C.sb([128, D]) for _ in range(NB)]
    tmp = [C.sb([128, D]) for _ in range(NB)]
    junk = C.sb([128, D])
    ss4 = [C.sb([128, 4]) for _ in range(NB)]
    ssa = [C.sb([128, 1]) for _ in range(NB)]
    ssb = [C.sb([128, 1]) for _ in range(NB)]
    if y_fm is not None:
        yf = [C.sb([128, 16, 128]) for _ in range(NB)]
        xn = [C.sb([128, D]) for _ in range(NB)]
        pY = [C.ps() for _ in range(4)]
        gm = C.sb([128, D])
        gtmp = C.sb([128, D])
        S.dma("sp", gm[:], gm_rows[0].partition_broadcast(128), w=[gm])
        S.dma("sp", gtmp[:], gm_rows[1].partition_broadcast(128), w=[gtmp])
        S.add("dve", lambda e: e.tensor_tensor(out=gm[:], in0=gm[:], in1=gtmp[:], op=ALU.mult), r=[gm, gtmp], w=[gm])
        yv = y_fm.rearrange("(c p) t -> p c t", p=128)
    else:
        xn = xt
    if h_fm is not None:
        hb = [C.sb([128, D], BF16) for _ in range(NB)]
        hT = [C.sb([128, 16, 128], BF16) for _ in range(NB)]
        pH = C.ps([128, 2048], BF16)
        gg = C.sb([128, D])
        sh = C.sb([128, D])
        g2 = C.sb([128, D])
        S.dma("sp", g2[:], h_rows[0].partition_broadcast(128), w=[g2])
        S.dma("sp", gg[:], h_rows[1].partition_broadcast(128), w=[gg])
        S.dma("sp", sh[:], h_rows[2].partition_broadcast(128), w=[sh])
        S.add("dve", lambda e: e.scalar_tensor_tensor(out=gg[:], in0=gg[:], scalar=1.0, in1=g2[:],
                                                     op0=ALU.add, op1=ALU.mult), r=[gg, g2], w=[gg])
        hv = h_fm.rearrange("(c p) t -> p c t", p=128)

    def rstd(ss, i):
        S.add("dve", lambda e: e.tensor_scalar(out=ss[:], in0=ss[:], scalar1=1.0 / D, scalar2=1e-6,
                                               op0=ALU.mult, op1=ALU.add), r=[ss], w=[ss])
        S.add("act", lambda e: e.activation(out=ss[:], in_=ss[:], func=AF.Sqrt), r=[ss], w=[ss])
        S.add("dve", lambda e: e.reciprocal(out=ss[:], in_=ss[:]), r=[ss], w=[ss])

    for i in range(ntok // 128):
        b = i % NB
        x_, t_, s4, sa, sb_ = xt[b], tmp[b], ss4[b], ssa[b], ssb[b]
        S.dma("sp", x_[:], x_in[i * 128:(i + 1) * 128, :], w=[x_])
        if y_fm is not None:
            y_, xn_ = yf[b], xn[b]
            S.dma("act", y_[:], yv[:, :, i * 128:(i + 1) * 128], w=[y_])
            for c in range(16):
                S.add("pe", (lambda c=c, y_=y_: lambda e: e.transpose(
                    out=pY[c // 4][:, (c % 4) * 128:(c % 4 + 1) * 128], in_=y_[:, c, :], identity=K["identF"][:]))(),
                    r=[y_, K["identF"]], w=[pY[c // 4]])
            for j in range(4):
                S.add("act", (lambda j=j, s4=s4: lambda e: e.activation(
                    out=junk[:, j * 512:(j + 1) * 512], in_=pY[j][:], func=AF.Square, accum_out=s4[:, j:j + 1]))(),
                    r=[pY[j]], w=[junk, s4])
            S.add("dve", (lambda s4=s4, sa=sa: lambda e: e.tensor_reduce(out=sa[:], in_=s4[:], axis=AX.X, op=ALU.add))(),
                  r=[s4], w=[sa])
            rstd(sa, i)
            for j in range(4):
                S.add("dve", (lambda j=j, t_=t_, sa=sa: lambda e: e.scalar_tensor_tensor(
                    out=t_[:, j * 512:(j + 1) * 512], in0=pY[j][:], scalar=sa[:, 0:1], in1=gm[:, j * 512:(j + 1) * 512],
                    op0=ALU.mult, op1=ALU.mult))(), r=[pY[j], sa, gm], w=[t_])
            S.add("pool", (lambda t_=t_, x_=x_, xn_=xn_: lambda e: e.tensor_tensor(
                out=xn_[:], in0=t_[:], in1=x_[:], op=ALU.add))(), r=[t_, x_], w=[xn_])
            if x_out is not None:
                S.dma("pool", x_out[i * 128:(i + 1) * 128, :], xn_[:], r=[xn_])
        else:
            xn_ = x_
        if h_fm is not None:
            h_, hT_ = hb[b], hT[b]
            S.add("act", (lambda xn_=xn_, sb_=sb_: lambda e: e.activation(
                out=junk[:], in_=xn_[:], func=AF.Square, accum_out=sb_[:, 0:1]))(), r=[xn_], w=[junk, sb_])
            rstd(sb_, i)
            S.add("dve", (lambda t_=t_, xn_=xn_, sb_=sb_: lambda e: e.scalar_tensor_tensor(
                out=t_[:], in0=xn_[:], scalar=sb_[:, 0:1], in1=gg[:], op0=ALU.mult, op1=ALU.mult))(),
                r=[xn_, sb_, gg], w=[t_])
            S.add("pool", (lambda t_=t_, h_=h_: lambda e: e.tensor_tensor(
                out=h_[:], in0=t_[:], in1=sh[:], op=ALU.add))(), r=[t_, sh], w=[h_])
            for c in range(16):
                S.add("pe", (lambda c=c, h_=h_: lambda e: e.transpose(
                    out=pH[:, c * 128:(c + 1) * 128], in_=h_[:, c * 128:(c + 1) * 128], identity=K["identB"][:]))(),
                    r=[h_, K["identB"]], w=[pH])
            S.add("act", (lambda hT_=hT_: lambda e: e.activation(
                out=hT_[:, 0:8, :], in_=pH[:, 0:1024], func=AF.Copy))(), r=[pH], w=[hT_])
            S.add("dve", (lambda hT_=hT_: lambda e: e.tensor_copy(
                out=hT_[:, 8:16, :], in_=pH[:, 1024:2048]))(), r=[pH], w=[hT_])
            for hb0 in hbases:
                S.dma("pool", hv[:, :, hb0 + i * 128:hb0 + (i + 1) * 128], hT_[:], r=[hT_])
    C.close()


def phase_consts(nc, S, K, identF, identB, bones, cols, mu):
    S.dma("sp", K["identF"][:], identF, w=[K["identF"]])
    S.dma("sp", K["identB"][:], identB, w=[K["identB"]])
    S.dma("sp", K["bones"][:], bones, w=[K["bones"]])
    S.dma("sp", K["cols"][:], cols, w=[K["cols"]])
    S.dma("sp", K["mu"][:], mu, w=[K["mu"]])
    S.add("dve", lambda e: e.tensor_scalar(out=K["omu"][:], in0=K["mu"][:], scalar1=-1.0, scalar2=1.0,
                                           op0=ALU.mult, op1=ALU.add), r=[K["mu"]], w=[K["omu"]])
    S.flush()


def phase_mix(nc, S, K, h_fm, xm):
    C = Ctx(nc, S)
    NB = 2
    hch = [C.sb([128, TT], BF16) for _ in range(NB)]
    hs = [C.sb([128, TT], BF16) for _ in range(NB)]
    tmp = [C.sb([128, TT], BF16) for _ in range(3)]
    outs = [C.sb([128, TT], BF16) for _ in range(3)]
    k = 0
    for c in range(16):
        b = c % NB
        h_, s_ = hch[b], hs[b]
        S.dma("sp", h_[:], h_fm[c * 128:(c + 1) * 128, :], w=[h_])
        S.add("pool", (lambda s_=s_: lambda e: e.memset(s_[:], 0.0))(), w=[s_])
        q = c // 4
        hl = h_[:, TC:TC + T].rearrange("p (r c) -> p r c", c=64)
        sl = s_[:, TC:TC + T].rearrange("p (r c) -> p r c", c=64)
        if q == 0:
            src, dst = hl[:, :, 0:63], sl[:, :, 1:64]
        elif q == 1:
            src, dst = hl[:, :, 1:64], sl[:, :, 0:63]
        elif q == 2:
            src, dst = hl[:, 0:31, :], sl[:, 1:32, :]
        else:
            src, dst = hl[:, 1:32, :], sl[:, 0:31, :]
        S.add("pool", (lambda src=src, dst=dst: lambda e: e.tensor_copy(out=dst, in_=src))(), r=[h_], w=[s_])
        for base in (0, TC + T):
            if c < 8:
                src, dst = h_[:, base:base + TC - 1], s_[:, base + 1:base + TC]
            else:
                src, dst = h_[:, base + 1:base + TC], s_[:, base:base + TC - 1]
            S.add("pool", (lambda src=src, dst=dst: lambda e: e.tensor_copy(out=dst, in_=src))(), r=[h_], w=[s_])
        for j in range(6):
            t_, o_ = tmp[k % 3], outs[k % 3]
            k += 1
            S.add("act", (lambda t_=t_, s_=s_, c=c, j=j: lambda e: e.activation(
                out=t_[:], in_=s_[:], func=AF.Copy, scale=K["mu"][:, c, j:j + 1]))(), r=[s_, K["mu"]], w=[t_])
            S.add("dve", (lambda t_=t_, o_=o_, h_=h_, c=c, j=j: lambda e: e.scalar_tensor_tensor(
                out=o_[:], in0=h_[:], scalar=K["omu"][:, c, j:j + 1], in1=t_[:], op0=ALU.mult, op1=ALU.add))(),
                r=[h_, t_, K["omu"]], w=[o_])
            S.dma("pool" if j % 2 else "act", xm[j][c * 128:(c + 1) * 128, :], o_[:], r=[o_])
    C.close()


def gemm_fm(nc, S, C, X, Kdim, NT, W, N, epi, msz=128, TB=None, xbf=True, nps=8):
    TB = TB or NT
    KC = (Kdim + 127) // 128
    ksz = [min(128, Kdim - kc * 128) for kc in range(KC)]
    Xs = C.sb([128, KC, TB], BF16, "Xs")
    wf = [C.sb([128, KC, msz], F32, "wf") for _ in range(2)]
    wb = [C.sb([128, KC, msz], BF16, "wb") for _ in range(2)]
    acc = [C.sb([128, TB], F32, "acc") for _ in range(2)]
    ps = [C.ps() for _ in range(nps)]
    it = 0
    pi = 0
    nblk = [(o, min(512, TB - o)) for o in range(0, TB, 512)]
    for tb in range(NT // TB):
        if Kdim % 128 == 0:
            xv = X.rearrange("(c p) t -> p c t", p=128)
            half = KC // 2 if KC >= 2 else KC
            S.dma("sp", Xs[:, 0:half, :], xv[:, 0:half, tb * TB:(tb + 1) * TB], w=[Xs])
            if half < KC:
                S.dma("act", Xs[:, half:KC, :], xv[:, half:KC, tb * TB:(tb + 1) * TB], w=[Xs])
        else:
            S.dma("sp", Xs[0:Kdim, 0, :], X[:, tb * TB:(tb + 1) * TB], w=[Xs])
        for m in range(N // msz):
            wf_, wb_, acc_ = wf[it % 2], wb[it % 2], acc[it % 2]
            if Kdim % 128 == 0:
                if len(W.shape) == 4:
                    S.dma("sp", wf_[:], W[m], w=[wf_])
                else:
                    S.dma("sp", wf_[:], W[:, m * msz:(m + 1) * msz].rearrange("(c p) n -> p c n", p=128), w=[wf_])
                if it % 2 == 0:
                    S.add("dve", (lambda wf_=wf_, wb_=wb_: lambda e: e.tensor_copy(out=wb_[:], in_=wf_[:]))(),
                          r=[wf_], w=[wb_])
                else:
                    S.add("act", (lambda wf_=wf_, wb_=wb_: lambda e: e.activation(out=wb_[:], in_=wf_[:], func=AF.Copy))(),
                          r=[wf_], w=[wb_])
            else:
                S.dma("sp", wf_[0:Kdim, 0, :], W[:, m * msz:(m + 1) * msz], w=[wf_])
                S.add("pool", (lambda wf_=wf_, wb_=wb_: lambda e: e.tensor_copy(
                    out=wb_[0:Kdim, 0, :], in_=wf_[0:Kdim, 0, :]))(), r=[wf_], w=[wb_])
            for (o, n) in nblk:
                p_ = ps[pi % nps]
                pi += 1
                for kc in range(KC):
                    S.add("pe", (lambda p_=p_, wb_=wb_, kc=kc, o=o, n=n: lambda e: e.matmul(
                        p_[0:msz, 0:n], lhsT=wb_[0:ksz[kc], kc, :], rhs=Xs[0:ksz[kc], kc, o:o + n],
                        start=(kc == 0), stop=(kc == KC - 1)))(), r=[wb_, Xs], w=[p_])
                if (pi % 2) == 0:
                    S.add("act", (lambda p_=p_, acc_=acc_, o=o, n=n: lambda e: e.activation(
                        out=acc_[0:msz, o:o + n], in_=p_[0:msz, 0:n], func=AF.Copy))(), r=[p_], w=[acc_])
                else:
                    S.add("dve", (lambda p_=p_, acc_=acc_, o=o, n=n: lambda e: e.tensor_copy(
                        out=acc_[0:msz, o:o + n], in_=p_[0:msz, 0:n]))(), r=[p_], w=[acc_])
            epi(tb, m, acc_, it)
            it += 1


def epi_store(S, dst, msz=128, TB=None):
    def epi(tb, m, acc, it):
        if TB is None:
            S.dma("pool", dst[m * msz:(m + 1) * msz, :], acc[0:msz, :], r=[acc])
        else:
            S.dma("pool", dst[m * msz:(m + 1) * msz, tb * TB:(tb + 1) * TB], acc[0:msz, :], r=[acc])
    return epi


def epi_act(S, C, dst, func, NT, odt=BF16, msz=128, bias=None, post=None):
    ot = [C.sb([128, NT], odt, "eo") for _ in range(2)]

    def epi(tb, m, acc, it):
        o_ = ot[it % 2]
        if bias is None:
            S.add("act", lambda e: e.activation(out=o_[0:msz, :], in_=acc[0:msz, :], func=func), r=[acc], w=[o_])
        else:
            bcol = bias(m)
            S.add("act", lambda e: e.activation(out=o_[0:msz, :], in_=acc[0:msz, :], func=func, bias=bcol),
                  r=[acc], w=[o_])
        if post is not None:
            S.add("dve", lambda e: e.tensor_scalar(out=o_[0:msz, :], in0=o_[0:msz, :], scalar1=post, scalar2=None,
                                                   op0=ALU.mult), r=[o_], w=[o_])
        S.dma("pool", dst[m * msz:(m + 1) * msz, :], o_[0:msz, :], r=[o_])
    return epi


def phase_rwkv_proj(nc, S, K, xm, P, Dr):
    col = lambda name: (lambda m: K["cols"][:, CI[name], m:m + 1])
    for (j, wname, dst) in ((0, "rw_w_r", "r"), (2, "rw_w_k", "k"), (3, "rw_w_v", "v")):
        C = Ctx(nc, S)
        gemm_fm(nc, S, C, xm[j], D, TT, P[wname], D, epi_store(S, Dr[dst]))
        C.close()
    for e in range(2):
        C = Ctx(nc, S)
        gemm_fm(nc, S, C, xm[1], D, TT, P["rw_dec_w1"][e], 96, epi_act(S, C, Dr["t1"], AF.Tanh, TT, msz=96), msz=96)
        C.close()
        C = Ctx(nc, S)
        gemm_fm(nc, S, C, Dr["t1"], 96, TT, P["rw_dec_w2"][e], D,
                epi_act(S, C, Dr["logw"][e], AF.Sigmoid, TT, odt=F32, bias=col("dec_w0_%d" % e), post=-0.6065306597126334))
        C.close()
        C = Ctx(nc, S)
        gemm_fm(nc, S, C, xm[4], D, TT, P["rw_a1"][e], 96, epi_act(S, C, Dr["t1"], AF.Copy, TT, msz=96), msz=96)
        C.close()
        C = Ctx(nc, S)
        gemm_fm(nc, S, C, Dr["t1"], 96, TT, P["rw_a2"][e], D,
                epi_act(S, C, Dr["a"][e], AF.Sigmoid, TT, odt=F32, bias=col("a0_%d" % e)))
        C.close()
    C = Ctx(nc, S)
    gemm_fm(nc, S, C, xm[5], D, TT, P["rw_g1"], 256, epi_act(S, C, Dr["t2"], AF.Sigmoid, TT))
    C.close()
    C = Ctx(nc, S)
    gemm_fm(nc, S, C, Dr["t2"], 256, TT, P["rw_g2"], D, epi_store(S, Dr["g"]))
    C.close()


def phase_scanprep(nc, S, K, Dr):
    C = Ctx(nc, S)
    kt, vt, rt = C.sb([128, TT]), C.sb([128, TT]), C.sb([128, TT])
    at = [C.sb([128, TT]) for _ in range(2)]
    kk, sq, rs, kkn = C.sb([128, TT]), C.sb([128, TT]), C.sb([128, TT]), C.sb([128, TT])
    kd = [C.sb([128, TT]) for _ in range(2)]
    bb = [C.sb([128, TT]) for _ in range(2)]
    tmp = C.sb([128, TT])
    ps = [C.ps() for _ in range(8)]
    col = lambda name, c: K["cols"][:, CI[name], c:c + 1]
    blk = [(o, 512) for o in range(0, TT, 512)]
    pi = 0
    for c in range(16):
        rows = slice(c * 128, (c + 1) * 128)
        S.dma("sp", kt[:], Dr["k"][rows, :], w=[kt])
        S.dma("act", vt[:], Dr["v"][rows, :], w=[vt])
        S.dma("sp", rt[:], Dr["r"][rows, :], w=[rt])
        S.dma("act", at[0][:], Dr["a"][0][rows, :], w=[at[0]])
        S.dma("sp", at[1][:], Dr["a"][1][rows, :], w=[at[1]])
        S.add("dve", (lambda c=c: lambda e: e.tensor_scalar(out=kk[:], in0=kt[:], scalar1=col("k_k", c), scalar2=None,
                                                          op0=ALU.mult))(), r=[kt, K["cols"]], w=[kk])
        S.add("act", lambda e: e.activation(out=sq[:], in_=kk[:], func=AF.Square), r=[kk], w=[sq])
        for (o, n) in blk:
            p_ = ps[pi % 8]
            pi += 1
            S.add("pe", (lambda p_=p_, o=o, n=n: lambda e: e.matmul(p_[:, 0:n], lhsT=K["bones"][:], rhs=sq[:, o:o + n],
                                                                    start=True, stop=True))(), r=[sq, K["bones"]], w=[p_])
            S.add("dve", (lambda p_=p_, o=o, n=n: lambda e: e.tensor_scalar(
                out=rs[:, o:o + n], in0=p_[:, 0:n], scalar1=1e-24, scalar2=None, op0=ALU.max))(), r=[p_], w=[rs])
        S.add("act", lambda e: e.activation(out=rs[:], in_=rs[:], func=AF.Sqrt), r=[rs], w=[rs])
        S.add("dve", lambda e: e.reciprocal(out=rs[:], in_=rs[:]), r=[rs], w=[rs])
        S.add("pool", lambda e: e.tensor_tensor(out=kkn[:], in0=kk[:], in1=rs[:], op=ALU.mult), r=[kk, rs], w=[kkn])
        S.dma("pool", Dr["kkn"][rows, :], kkn[:], r=[kkn])
        for e_ in range(2):
            S.add("dve", (lambda e_=e_, c=c: lambda e: e.tensor_scalar(
                out=kd[e_][:], in0=at[e_][:], scalar1=-1.0, scalar2=col("k_a", c), op0=ALU.add, op1=ALU.mult))(),
                r=[at[e_], K["cols"]], w=[kd[e_]])
            S.add("dve", (lambda e_=e_: lambda e: e.scalar_tensor_tensor(
                out=kd[e_][:], in0=kd[e_][:], scalar=1.0, in1=kt[:], op0=ALU.add, op1=ALU.mult))(),
                r=[kd[e_], kt], w=[kd[e_]])
            S.dma("pool", Dr["kd"][e_][rows, :], kd[e_][:], r=[kd[e_]])
            S.add("pool", (lambda e_=e_: lambda e: e.tensor_tensor(out=bb[e_][:], in0=kkn[:], in1=at[e_][:], op=ALU.mult))(),
                  r=[kkn, at[e_]], w=[bb[e_]])
            S.dma("pool", Dr["b"][e_][rows, :], bb[e_][:], r=[bb[e_]])
        S.add("pool", lambda e: e.tensor_tensor(out=tmp[:], in0=kd[0][:], in1=kd[1][:], op=ALU.add), r=[kd[0], kd[1]], w=[tmp])
        S.add("dve", (lambda c=c: lambda e: e.scalar_tensor_tensor(
            out=tmp[:], in0=tmp[:], scalar=col("r_k", c), in1=rt[:], op0=ALU.mult, op1=ALU.mult))(),
            r=[tmp, rt, K["cols"]], w=[tmp])
        for (o, n) in blk:
            p_ = ps[pi % 8]
            pi += 1
            S.add("pe", (lambda p_=p_, o=o, n=n: lambda e: e.matmul(p_[:, 0:n], lhsT=K["bones"][:], rhs=tmp[:, o:o + n],
                                                                    start=True, stop=True))(), r=[tmp, K["bones"]], w=[p_])
            S.add("dve", (lambda p_=p_, o=o, n=n: lambda e: e.tensor_tensor(
                out=sq[:, o:o + n], in0=p_[:, 0:n], in1=vt[:, o:o + n], op=ALU.mult))(), r=[p_, vt], w=[sq])
        S.dma("pool", Dr["bonus"][rows, :], sq[:], r=[sq])
    C.close()


def phase_scan(nc, S, K, Dr, Odram, scan_consts, TDT=F32, nsteps=18, lvl=9, dbgd=None):
    C = Ctx(nc, S)
    sb = C.sb
    NHP = 4
    names = ("logw", "kkn", "r", "b", "kd", "v")
    SETS = []
    for _ in range(2):
        d = {n: sb([128, NHP, 128]) for n in names}
        d.update(cs=sb([128, NHP, 128]), G1=sb([128, NHP, 128]), tot=sb([128, NHP, 1]), gamC=sb([128, NHP, 1]),
                 AR=sb([128, NHP, 256], BF16), Bt=sb([128, NHP, 128], BF16), Kt=sb([128, NHP, 128], BF16),
                 Bt32=sb([128, NHP, 128]), At_tm=sb([128, NHP, 128], BF16), Bh_tm=sb([128, NHP, 128], BF16),
                 Kh_tm=sb([128, NHP, 128], BF16), V_tm=sb([128, NHP, 128], BF16), Rp=sb([128, NHP, 128]),
                 ACt=sb([128, NHP, 64]), Of=sb([128, NHP, 128]),
                 rRp=[Res() for _ in range(NHP)], rAC=[Res() for _ in range(NHP)], rOf=[Res() for _ in range(NHP)])
        SETS.append(d)
    ST = [sb([128, 16, 64]) for _ in range(2)]
    rST = [[Res() for _ in range(16)] for _ in range(2)]
    mask01 = sb([128, NHP, 128])
    MASK = [sb([128, 256]) for _ in range(2)]
    ID2 = sb([128, 64])
    MD = [sb([128, 128]) for _ in range(2)]
    M1c = [sb([128, 128]) for _ in range(2)]
    M2c = [sb([128, 128]) for _ in range(2)]
    MTD = [sb([128, 128]) for _ in range(2)]
    G = 8
    W = []
    for _ in range(G):
        W.append({"Nm": sb([128, 128], TDT), "NTm": sb([128, 128], TDT), "N1": sb([128, 128], BF16), "N2": sb([128, 128], BF16),
                  "Z": sb([128, 128], BF16), "Za": sb([128, 128], BF16), "Zc": sb([128, 128], BF16), "Z1": sb([128, 128], BF16),
                  "Zw": sb([128, 128], BF16), "Xb": sb([128, 128], BF16), "Tm": [sb([128, 128], TDT) for _ in range(2)],
                  "P": [sb([128, 128], TDT) for _ in range(2)], "PT": [sb([128, 128], TDT) for _ in range(2)],
                  "Mbr": sb([128, 128], BF16), "Mka": sb([128, 128], BF16), "Mkr": sb([128, 128], BF16), "Nf": sb([128, 128], TDT),
                  "U1": sb([128, 64], BF16), "ApT": sb([128, 64], BF16)})
    banks = [C.ps() for _ in range(8)]
    PU = [Res(excl=True) for _ in range(8)]

    S.dma("sp", mask01[:], scan_consts["mask01"][:, 0:NHP, :], w=[mask01])
    for e_ in range(2):
        S.dma("sp", MASK[e_][:], scan_consts["mask"][e_], w=[MASK[e_]])
        S.dma("sp", MD[e_][:], scan_consts["md"][e_], w=[MD[e_]])
        S.dma("sp", M1c[e_][:], scan_consts["m1c"][e_], w=[M1c[e_]])
        S.dma("sp", M2c[e_][:], scan_consts["m2c"][e_], w=[M2c[e_]])
        S.dma("sp", MTD[e_][:], scan_consts["mtd"][e_], w=[MTD[e_]])
        S.add("pool", (lambda e_=e_: lambda e: e.memset(ST[e_][:], 0.0))(), w=rST[e_])
    S.dma("sp", ID2[:], scan_consts["id2"], w=[ID2])
    identF = K["identF"]
    flat = lambda t: t[:].rearrange("p a b -> p (a b)")
    BC = [128, NHP, 128]
    tcount = [0]

    def mm(out, lhsT, rhs, r, w, start=True, stop=True):
        S.add("pe", lambda e: e.matmul(out, lhsT=lhsT, rhs=rhs, start=start, stop=stop), r=r, w=w)

    def ev(eng, out, in0, in1, op, r, w):
        if in1 is None:
            if eng == "act":
                S.add("act", lambda e: e.activation(out=out, in_=in0, func=AF.Copy), r=r, w=w)
            else:
                S.add(eng, lambda e: e.tensor_copy(out=out, in_=in0), r=r, w=w)
        else:
            S.add(eng, lambda e: e.tensor_tensor(out=out, in0=in0, in1=in1, op=op), r=r, w=w)

    def prep(unit, d):
        s, e_, hh = unit
        c = s if e_ == 0 else 19 - s
        cols = slice(c * 128, (c + 1) * 128)
        srcs = {"logw": Dr["logw"][e_], "kkn": Dr["kkn"], "r": Dr["r"], "b": Dr["b"][e_], "kd": Dr["kd"][e_], "v": Dr["v"]}
        for qi, n in enumerate(names):
            S.dma("sp" if qi % 2 == 0 else "act", d[n][:],
                  srcs[n].rearrange("(c p) t -> p c t", p=128)[:, hh * NHP:(hh + 1) * NHP, cols], w=[d[n]])
        yield
        lw, kkn, r_, b_, kd_, v_ = (d[n] for n in names)
        cs, G1, tot, gamC, AR, Bt, Kt, Bt32 = d["cs"], d["G1"], d["tot"], d["gamC"], d["AR"], d["Bt"], d["Kt"], d["Bt32"]
        S.add("dve", lambda e: e.tensor_tensor_scan(out=flat(cs), data0=flat(mask01), data1=flat(lw), initial=0.0,
                                                    op0=ALU.mult, op1=ALU.add), r=[mask01, lw], w=[cs])
        yield
        if e_ == 1:
            S.add("pool", lambda e: e.tensor_copy(out=tot[:], in_=cs[:, :, 127:128]), r=[cs], w=[tot])
            S.add("pool", lambda e: e.tensor_tensor(out=G1[:], in0=lw[:], in1=cs[:], op=ALU.subtract), r=[lw, cs], w=[G1])
            yield
            S.add("pool", lambda e: e.tensor_tensor(out=cs[:], in0=G1[:], in1=tot[:].to_broadcast(BC), op=ALU.add),
                  r=[G1, tot], w=[cs])
            yield
        S.add("act", lambda e: e.activation(out=G1[:], in_=cs[:], func=AF.Exp), r=[cs], w=[G1])
        S.add("pool", lambda e: e.tensor_tensor(out=lw[:], in0=cs[:], in1=lw[:], op=ALU.subtract), r=[cs, lw], w=[lw])
        yield
        S.add("act", lambda e: e.activation(out=lw[:], in_=lw[:], func=AF.Exp), r=[lw], w=[lw])
        gsl = G1[:, :, 127:128] if e_ == 0 else G1[:, :, 0:1]
        S.add("pool", lambda e: e.tensor_copy(out=gamC[:], in_=gsl), r=[G1], w=[gamC])
        S.add("act", lambda e: e.activation(out=cs[:], in_=cs[:], func=AF.Exp, scale=-1.0), r=[cs], w=[cs])
        yield
        S.add("dve", lambda e: e.scalar_tensor_tensor(out=kkn[:], in0=kkn[:], scalar=-1.0, in1=lw[:],
                                                      op0=ALU.mult, op1=ALU.mult), r=[kkn, lw], w=[kkn])
        S.add("pool", lambda e: e.tensor_copy(out=AR[:, :, 0:128], in_=kkn[:]), r=[kkn], w=[AR])
        yield
        S.add("pool", lambda e: e.tensor_tensor(out=r_[:], in0=r_[:], in1=G1[:], op=ALU.mult), r=[r_, G1], w=[r_])
        S.add("act", lambda e: e.activation(out=AR[:, :, 128:256], in_=r_[:], func=AF.Copy), r=[r_], w=[AR])
        yield
        S.add("dve", lambda e: e.tensor_tensor(out=b_[:], in0=b_[:], in1=cs[:], op=ALU.mult), r=[b_, cs], w=[b_])
        S.add("act", lambda e: e.activation(out=Bt[:], in_=b_[:], func=AF.Copy), r=[b_], w=[Bt])
        S.add("pool", lambda e: e.tensor_copy(out=Bt32[:], in_=b_[:]), r=[b_], w=[Bt32])
        yield
        S.add("pool", lambda e: e.tensor_tensor(out=b_[:], in0=b_[:], in1=gamC[:].to_broadcast(BC), op=ALU.mult),
              r=[b_, gamC], w=[b_])
        yield
        S.add("pool", lambda e: e.tensor_tensor(out=kd_[:], in0=kd_[:], in1=cs[:], op=ALU.mult), r=[kd_, cs], w=[kd_])
        S.add("act", lambda e: e.activation(out=Kt[:], in_=kd_[:], func=AF.Copy), r=[kd_], w=[Kt])
        yield
        S.add("pool", lambda e: e.tensor_tensor(out=kd_[:], in0=kd_[:], in1=gamC[:].to_broadcast(BC), op=ALU.mult),
              r=[kd_, gamC], w=[kd_])
        yield
        for (src, dst) in ((kkn, d["At_tm"]), (b_, d["Bh_tm"]), (kd_, d["Kh_tm"]), (v_, d["V_tm"])):
            for g in range(NHP // 4):
                bk = tcount[0] % 8
                tcount[0] += 1
                for j in range(4):
                    hp = g * 4 + j
                    S.add("pe", (lambda src=src, hp=hp, bk=bk, j=j: lambda e: e.transpose(
                        out=banks[bk][:, j * 128:(j + 1) * 128], in_=src[:, hp, :], identity=identF[:]))(),
                        r=[src, identF], w=[PU[bk]])
                S.add("act", (lambda dst=dst, g=g, bk=bk: lambda e: e.activation(
                    out=dst[:, g * 4:(g + 1) * 4, :], in_=banks[bk][:, :], func=AF.Copy))(), r=[PU[bk]], w=[dst])
                yield

    def heads(unit, d, grp):
        s, e_, hh = unit
        c = s if e_ == 0 else 19 - s
        need_o = 2 <= c <= 17
        mk = MASK[e_]
        kkn, r_ = d["kkn"], d["r"]
        AR, Bt, Kt, Bt32 = d["AR"], d["Bt"], d["Kt"], d["Bt32"]
        At_tm, Bh_tm, Kh_tm, V_tm = d["At_tm"], d["Bh_tm"], d["Kh_tm"], d["V_tm"]
        Rp, ACt, Of, gamC = d["Rp"], d["ACt"], d["Of"], d["gamC"]
        HD = []
        for gi in range(G):
            hidx = grp * G + gi
            hp, half = hidx // 2, hidx % 2
            hs = slice(half * 64, (half + 1) * 64)
            HD.append(dict(hp=hp, gp=hh * NHP + hp, hs=hs, hc=hs, w=W[gi], bk=banks[gi], rb=[PU[gi]]))
        for h in HD:
            hp, hs, bk = h["hp"], h["hs"], h["bk"]
            mm(bk[:, 0:128], Bt32[hs, hp, :], kkn[hs, hp, :], [Bt32, kkn], h["rb"])
            mm(bk[:, 128:256], kkn[hs, hp, :], Bt32[hs, hp, :], [Bt32, kkn], h["rb"])
            mm(bk[:, 256:384], Bt[hs, hp, :], AR[hs, hp, 128:256], [Bt, AR], h["rb"])
            mm(bk[:, 384:512], Kt[hs, hp, :], AR[hs, hp, 0:128], [Kt, AR], h["rb"])
        yield False
        for h in HD:
            w_, bk = h["w"], h["bk"]
            ev("act", w_["Nf"][:], bk[:, 0:128], None, None, h["rb"], [w_["Nf"]])
            ev("dve", w_["NTm"][:], bk[:, 128:256], MTD[e_][:], ALU.mult, h["rb"] + [MTD[e_]], [w_["NTm"]])
            ev("dve", w_["Mbr"][:], bk[:, 256:384], mk[:, 128:256], ALU.mult, h["rb"] + [mk], [w_["Mbr"]])
            ev("dve", w_["Mka"][:], bk[:, 384:512], mk[:, 0:128], ALU.mult, h["rb"] + [mk], [w_["Mka"]])
        yield True
        for h in HD:
            w_ = h["w"]
            ev("pool", w_["Nm"][:], w_["Nf"][:], MD[e_][:], ALU.mult, [w_["Nf"], MD[e_]], [w_["Nm"]])
            ev("pool", w_["Tm"][0][:], w_["Nm"][:], identF[:], ALU.add, [w_["Nm"], identF], [w_["Tm"][0]])
            ev("pool", w_["N1"][:], w_["Nf"][:], M1c[e_][:], ALU.mult, [w_["Nf"], M1c[e_]], [w_["N1"]])
            ev("pool", w_["N2"][:], w_["Nf"][:], M2c[e_][:], ALU.mult, [w_["Nf"], M2c[e_]], [w_["N2"]])
            h["P"], h["PT"], h["T"] = w_["Nm"], w_["NTm"], w_["Tm"][0]
        yield True
        for lv in range(4):
            last = lv == 3
            for h in HD:
                bk = h["bk"]
                mm(bk[:, 0:128], h["P"][:], h["PT"][:], [h["P"], h["PT"]], h["rb"])
                if not last:
                    mm(bk[:, 128:256], h["PT"][:], h["P"][:], [h["P"], h["PT"]], h["rb"])
            yield False
            for h in HD:
                w_, bk = h["w"], h["bk"]
                Pn, PTn = w_["P"][lv % 2], w_["PT"][lv % 2]
                ev("act", PTn[:], bk[:, 0:128], None, None, h["rb"], [PTn])
                if not last:
                    ev("act", Pn[:], bk[:, 128:256], None, None, h["rb"], [Pn])
                h["P"], h["PT"] = Pn, PTn
            yield True
            for h in HD:
                mm(h["bk"][:, 256:384], h["PT"][:], h["T"][:], [h["PT"], h["T"]], h["rb"])
            yield False
            for h in HD:
                w_ = h["w"]
                Tn = w_["Xb"] if last else w_["Tm"][(lv + 1) % 2]
                ev("dve", Tn[:], h["bk"][:, 256:384], h["T"][:], ALU.add, h["rb"] + [h["T"]], [Tn])
                h["T"] = Tn
            yield True
        for h in HD:
            mm(h["bk"][:, 384:448], h["w"]["Mka"][:], V_tm[:, h["hp"], h["hc"]], [h["w"]["Mka"], V_tm], h["rb"])
        yield False
        for h in HD:
            w_ = h["w"]
            ev("act", w_["Z"][:, 0:64], h["bk"][:, 384:448], None, None, h["rb"], [w_["Z"]])
            ev("pool", w_["Z"][:, 64:128], At_tm[:, h["hp"], h["hc"]], None, None, [At_tm], [w_["Z"]])
        yield True

        def apply64(srck, dstk):
            for h in HD:
                mm(h["bk"][:, 0:128], h["T"][:], h["w"][srck][:], [h["T"], h["w"][srck]], h["rb"])
            yield False
            for h in HD:
                ev("act", h["w"]["Za"][:], h["bk"][:, 0:128], None, None, h["rb"], [h["w"]["Za"]])
            yield True
            for h in HD:
                mm(h["bk"][:, 128:256], h["w"]["N1"][:], h["w"]["Za"][:], [h["w"]["N1"], h["w"]["Za"]], h["rb"])
            yield False
            for h in HD:
                ev("act", h["w"]["Zc"][:], h["bk"][:, 128:256], None, None, h["rb"], [h["w"]["Zc"]])
            yield True
            for h in HD:
                mm(h["bk"][:, 256:384], h["T"][:], h["w"]["Zc"][:], [h["T"], h["w"]["Zc"]], h["rb"])
            yield False
            for h in HD:
                ev("dve", h["w"][dstk][:], h["bk"][:, 256:384], h["w"]["Za"][:], ALU.add,
                   h["rb"] + [h["w"]["Za"]], [h["w"][dstk]])
            yield True

        yield from apply64("Z", "Z1")
        for h in HD:
            mm(h["bk"][:, 384:512], h["w"]["N2"][:], h["w"]["Z1"][:], [h["w"]["N2"], h["w"]["Z1"]], h["rb"])
        yield False
        for h in HD:
            ev("act", h["w"]["Zw"][:], h["bk"][:, 384:512], None, None, h["rb"], [h["w"]["Zw"]])
        yield True
        yield from apply64("Zw", "Z")
        for h in HD:
            w_ = h["w"]
            ev("pool", w_["U1"][:], w_["Z"][:, 0:64], w_["Z1"][:, 0:64], ALU.add, [w_["Z"], w_["Z1"]], [w_["U1"]])
            ev("pool", w_["ApT"][:], w_["Z"][:, 64:128], w_["Z1"][:, 64:128], ALU.add, [w_["Z"], w_["Z1"]], [w_["ApT"]])
        yield True
        for h in HD:
            hp, hs, hc, w_, bk = h["hp"], h["hs"], h["hc"], h["w"], h["bk"]
            mm(bk[hs, 0:64], w_["ApT"][:], Bh_tm[:, hp, hc], [w_["ApT"], Bh_tm], h["rb"])
            if need_o:
                mm(bk[hs, 128:256], w_["ApT"][:], w_["Mbr"][:], [w_["ApT"], w_["Mbr"]], h["rb"])
                mm(bk[:, 384:512], Kt[hs, hp, :], AR[hs, hp, 128:256], [Kt, AR], h["rb"])
        yield False
        for h in HD:
            hp, hs, bk, w_ = h["hp"], h["hs"], h["bk"], h["w"]
            S.add("dve", (lambda bk=bk, hs=hs, hp=hp: lambda e: e.scalar_tensor_tensor(
                out=ACt[hs, hp, :], in0=ID2[hs, :], scalar=gamC[hs, hp, 0:1], in1=bk[hs, 0:64],
                op0=ALU.mult, op1=ALU.add))(), r=h["rb"] + [ID2, gamC], w=[d["rAC"][hp]])
            if need_o:
                ev("dve", Rp[hs, hp, :], bk[hs, 128:256], r_[hs, hp, :], ALU.add, h["rb"] + [r_], [d["rRp"][hp]])
                ev("dve", w_["Mkr"][:], bk[:, 384:512], mk[:, 128:256], ALU.mult, h["rb"] + [mk], [w_["Mkr"]])
        yield True
        for h in HD:
            hp, gp, hs, hc, w_, bk = h["hp"], h["gp"], h["hs"], h["hc"], h["w"], h["bk"]
            if need_o:
                pO = bk[hs, 256:384]
                mm(pO, w_["U1"][:], w_["Mbr"][:], [w_["U1"], w_["Mbr"]], h["rb"], start=True, stop=False)
                mm(pO, V_tm[:, hp, hc], w_["Mkr"][:], [V_tm, w_["Mkr"]], h["rb"], start=False, stop=False)
                mm(pO, ST[e_][hs, gp, :], Rp[hs, hp, :], [rST[e_][gp], d["rRp"][hp]], h["rb"], start=False, stop=True)
            pS = bk[hs, 64:128]
            mm(pS, Bh_tm[:, hp, hc], w_["U1"][:], [Bh_tm, w_["U1"]], h["rb"], start=True, stop=False)
            mm(pS, Kh_tm[:, hp, hc], V_tm[:, hp, hc], [Kh_tm, V_tm], h["rb"], start=False, stop=False)
            mm(pS, ACt[hs, hp, :], ST[e_][hs, gp, :], [d["rAC"][hp], rST[e_][gp]], h["rb"], start=False, stop=True)
        yield False
        for h in HD:
            hp, gp, hs, bk = h["hp"], h["gp"], h["hs"], h["bk"]
            if need_o:
                ev("act", Of[hs, hp, :], bk[hs, 256:384], None, None, h["rb"], [d["rOf"][hp]])
            ev("act", ST[e_][hs, gp, :], bk[hs, 64:128], None, None, h["rb"], [rST[e_][gp]])
        yield True

    units = [(s, e_, hh) for s in range(nsteps) for e_ in range(2) for hh in range(16 // NHP)]
    NG = (2 * NHP) // G
    for _ in prep(units[0], SETS[0]):
        pass
    for ui, u in enumerate(units):
        d = SETS[ui % 2]
        nxt = prep(units[ui + 1], SETS[(ui + 1) % 2]) if ui + 1 < len(units) else None
        for grp in range(NG if lvl >= 3 else 0):
            for safe in heads(u, d, grp):
                if safe and nxt is not None:
                    try:
                        next(nxt)
                        next(nxt)
                    except StopIteration:
                        nxt = None
        if nxt is not None:
            for _ in nxt:
                pass
        s, e_, hh = u
        c = s if e_ == 0 else 19 - s
        if 2 <= c <= 17 and lvl >= 7:
            S.dma("pool", Odram[e_].rearrange("(c p) t -> p c t", p=128)[:, hh * NHP:(hh + 1) * NHP, (c - 2) * 128:(c - 1) * 128],
                  d["Of"][:], r=d["rOf"])
    C.close()


def phase_readout(nc, S, K, Dr, Odram, XO):
    C = Ctx(nc, S)
    o0, o1, bn, gt = C.sb([128, T]), C.sb([128, T]), C.sb([128, T]), C.sb([128, T])
    mean, sq, rstd = C.sb([128, T]), C.sb([128, T]), C.sb([128, T])
    xo = [C.sb([128, T], BF16) for _ in range(2)]
    ps = [C.ps() for _ in range(8)]
    col = lambda name, c: K["cols"][:, CI[name], c:c + 1]
    pi = 0
    for c in range(16):
        rows = slice(c * 128, (c + 1) * 128)
        S.dma("sp", o0[:], Odram[0][rows, :], w=[o0])
        S.dma("act", o1[:], Odram[1][rows, :], w=[o1])
        S.dma("sp", bn[:], Dr["bonus"][rows, TC:TC + T], w=[bn])
        S.dma("act", gt[:], Dr["g"][rows, TC:TC + T], w=[gt])
        S.add("pool", lambda e: e.tensor_tensor(out=o0[:], in0=o0[:], in1=o1[:], op=ALU.add), r=[o0, o1], w=[o0])
        for nb in range(4):
            p_ = ps[pi % 8]
            pi += 1
            S.add("pe", (lambda p_=p_, nb=nb: lambda e: e.matmul(p_[:, :], lhsT=K["bones"][:], rhs=o0[:, nb * 512:(nb + 1) * 512],
                                                               start=True, stop=True))(), r=[o0, K["bones"]], w=[p_])
            S.add("dve", (lambda p_=p_, nb=nb: lambda e: e.tensor_scalar(
                out=mean[:, nb * 512:(nb + 1) * 512], in0=p_[:, :], scalar1=1.0 / 64, scalar2=None, op0=ALU.mult))(),
                r=[p_], w=[mean])
        S.add("pool", lambda e: e.tensor_tensor(out=o0[:], in0=o0[:], in1=mean[:], op=ALU.subtract), r=[o0, mean], w=[o0])
        S.add("act", lambda e: e.activation(out=sq[:], in_=o0[:], func=AF.Square), r=[o0], w=[sq])
        for nb in range(4):
            p_ = ps[pi % 8]
            pi += 1
            S.add("pe", (lambda p_=p_, nb=nb: lambda e: e.matmul(p_[:, :], lhsT=K["bones"][:], rhs=sq[:, nb * 512:(nb + 1) * 512],
                                                               start=True, stop=True))(), r=[sq, K["bones"]], w=[p_])
            S.add("dve", (lambda p_=p_, nb=nb: lambda e: e.tensor_scalar(
                out=rstd[:, nb * 512:(nb + 1) * 512], in0=p_[:, :], scalar1=1.0 / 64, scalar2=64e-5,
                op0=ALU.mult, op1=ALU.add))(), r=[p_], w=[rstd])
        S.add("act", lambda e: e.activation(out=rstd[:], in_=rstd[:], func=AF.Sqrt), r=[rstd], w=[rstd])
        S.add("dve", lambda e: e.reciprocal(out=rstd[:], in_=rstd[:]), r=[rstd], w=[rstd])
        S.add("pool", lambda e: e.tensor_tensor(out=o0[:], in0=o0[:], in1=rstd[:], op=ALU.mult), r=[o0, rstd], w=[o0])
        S.add("dve", (lambda c=c: lambda e: e.tensor_scalar(out=o0[:], in0=o0[:], scalar1=col("lnx_w", c),
                                                          scalar2=col("lnx_b", c), op0=ALU.mult, op1=ALU.add))(),
              r=[o0, K["cols"]], w=[o0])
        S.add("pool", lambda e: e.tensor_tensor(out=o0[:], in0=o0[:], in1=bn[:], op=ALU.add), r=[o0, bn], w=[o0])
        x_ = xo[c % 2]
        S.add("dve", (lambda x_=x_: lambda e: e.tensor_tensor(out=x_[:], in0=o0[:], in1=gt[:], op=ALU.mult))(),
              r=[o0, gt], w=[x_])
        S.dma("pool", XO[rows, :], x_[:], r=[x_])
    C.close()


def epi_relu2(S, C, dst, NT):
    t32 = [C.sb([128, NT], F32, "r2") for _ in range(2)]
    ot = [C.sb([128, NT], BF16, "r2o") for _ in range(2)]

    def epi(tb, m, acc, it):
        t_, o_ = t32[it % 2], ot[it % 2]
        S.add("act", lambda e: e.activation(out=t_[:], in_=acc[:], func=AF.Relu), r=[acc], w=[t_])
        S.add("dve", lambda e: e.tensor_tensor(out=o_[:], in0=t_[:], in1=t_[:], op=ALU.mult), r=[t_], w=[o_])
        S.dma("pool", dst[m * 128:(m + 1) * 128, :], o_[:], r=[o_])
    return epi


def phase_mlp(nc, S, K, H, w_up, w_down, HID, YFM):
    C = Ctx(nc, S)
    gemm_fm(nc, S, C, H, D, T, w_up, DFF, epi_relu2(S, C, HID, T))
    C.close()
    C = Ctx(nc, S)
    gemm_fm(nc, S, C, HID, DFF, T, w_down, D, epi_store(S, YFM, TB=512), TB=512)
    C.close()


def phase_hy_in(nc, S, K, H, hy_in_w, TM3):
    C = Ctx(nc, S)
    zt = [C.sb([128, T + 2], F32, "zt") for _ in range(2)]
    o32 = [C.sb([128, T], F32, "o32") for _ in range(2)]
    ob = [C.sb([128, T], BF16, "ob") for _ in range(2)]
    oT = [C.sb([128, 16, 128], BF16, "oT") for _ in range(2)]
    pH = C.ps([128, 2048], BF16)
    col = lambda name, c: K["cols"][:, CI[name], c:c + 1]
    for z_ in zt:
        S.add("pool", (lambda z_=z_: lambda e: e.memset(z_[:], 0.0))(), w=[z_])

    def epi(tb, m, acc, it):
        s_, c = m // 16, m % 16
        z_, o_, b_, t_ = zt[it % 2], o32[it % 2], ob[it % 2], oT[it % 2]
        S.add("act", lambda e: e.activation(out=z_[:, 1:T + 1], in_=acc[:], func=AF.Identity, bias=col("hy_in_b%d" % s_, c)),
              r=[acc, K["cols"]], w=[z_])
        S.add("dve", lambda e: e.tensor_scalar(out=o_[:], in0=z_[:, 0:T], scalar1=col("cw0_%d" % s_, c),
                                               scalar2=col("cb_%d" % s_, c), op0=ALU.mult, op1=ALU.add), r=[z_, K["cols"]], w=[o_])
        S.add("dve", lambda e: e.scalar_tensor_tensor(out=o_[:], in0=z_[:, 1:T + 1], scalar=col("cw1_%d" % s_, c), in1=o_[:],
                                                      op0=ALU.mult, op1=ALU.add), r=[z_, o_, K["cols"]], w=[o_])
        S.add("dve", lambda e: e.scalar_tensor_tensor(out=b_[:], in0=z_[:, 2:T + 2], scalar=col("cw2_%d" % s_, c), in1=o_[:],
                                                      op0=ALU.mult, op1=ALU.add), r=[z_, o_, K["cols"]], w=[b_])
        for tc in range(16):
            S.add("pe", (lambda tc=tc: lambda e: e.transpose(out=pH[:, tc * 128:(tc + 1) * 128], in_=b_[:, tc * 128:(tc + 1) * 128],
                                                             identity=K["identB"][:]))(), r=[b_, K["identB"]], w=[pH])
        S.add("act", lambda e: e.activation(out=t_[:, 0:8, :], in_=pH[:, 0:1024], func=AF.Copy), r=[pH], w=[t_])
        S.add("dve", lambda e: e.tensor_copy(out=t_[:, 8:16, :], in_=pH[:, 1024:2048]), r=[pH], w=[t_])
        S.dma("pool", TM3[s_].rearrange("(tc p) d -> p tc d", p=128)[:, :, c * 128:(c + 1) * 128], t_[:], r=[t_])

    gemm_fm(nc, S, C, H, D, T, hy_in_w, 3 * D, epi, nps=6)
    C.close()


PI = 3.141592
TWO_PI = 6.283185307179586


def phase_hy_filt(nc, S, K, HC, HS, HD):
    C = Ctx(nc, S)
    z0 = C.sb([33, T])
    w1 = C.sb([33, 64])
    w23 = C.sb([64, 2, 64])
    w4 = C.sb([64, 4 * D])
    fbq = C.sb([64, 6])
    za, zb, msk = C.sb([64, T]), C.sb([64, T]), C.sb([64, T])
    delta = C.sb([128, D])
    negt = C.sb([128, 16])
    win = C.sb([128, D])
    fr = [C.sb([128, D]) for _ in range(4)]
    osd = [C.sb([128, D], BF16) for _ in range(4)]
    ps = [C.ps() for _ in range(8)]
    S.dma("sp", z0[:], HC["z0T"], w=[z0])
    S.dma("sp", w1[:], HC["f_w1"], w=[w1])
    S.dma("sp", w23[:], HC["f_w23"].rearrange("m k n -> k m n"), w=[w23])
    S.dma("sp", w4[:], HC["f_w4"], w=[w4])
    S.dma("sp", fbq[:], HC["fbq"], w=[fbq])
    S.dma("sp", delta[:], HC["delta"].partition_broadcast(128), w=[delta])
    S.dma("sp", negt[:], HC["negt"], w=[negt])
    cur = z0
    kdim = 33
    for layer in range(3):
        wl = w1[:, :] if layer == 0 else w23[:, layer - 1, :]
        wres = w1 if layer == 0 else w23
        nxt = za if layer % 2 == 0 else zb
        for nb in range(4):
            p_ = ps[nb]
            S.add("pe", (lambda p_=p_, nb=nb, wl=wl, cur=cur, kdim=kdim: lambda e: e.matmul(
                p_[0:64, :], lhsT=wl, rhs=cur[0:kdim, nb * 512:(nb + 1) * 512], start=True, stop=True))(),
                r=[wres, cur], w=[p_])
            S.add("dve", (lambda p_=p_, nb=nb, nxt=nxt, layer=layer: lambda e: e.tensor_scalar(
                out=nxt[:, nb * 512:(nb + 1) * 512], in0=p_[0:64, :], scalar1=fbq[:, layer:layer + 1],
                scalar2=fbq[:, 3 + layer:4 + layer], op0=ALU.add, op1=ALU.mult))(), r=[p_, fbq], w=[nxt])
        S.add("dve", (lambda nxt=nxt: lambda e: e.tensor_scalar(out=msk[:], in0=nxt[:], scalar1=PI, scalar2=-TWO_PI,
                                                               op0=ALU.is_gt, op1=ALU.mult))(), r=[nxt], w=[msk])
        S.add("dve", (lambda nxt=nxt: lambda e: e.tensor_tensor(out=nxt[:], in0=nxt[:], in1=msk[:], op=ALU.add))(),
              r=[nxt, msk], w=[nxt])
        S.add("dve", (lambda nxt=nxt: lambda e: e.tensor_scalar(out=msk[:], in0=nxt[:], scalar1=-PI, scalar2=TWO_PI,
                                                               op0=ALU.is_lt, op1=ALU.mult))(), r=[nxt], w=[msk])
        S.add("dve", (lambda nxt=nxt: lambda e: e.tensor_tensor(out=nxt[:], in0=nxt[:], in1=msk[:], op=ALU.add))(),
              r=[nxt, msk], w=[nxt])
        S.add("dve", (lambda nxt=nxt: lambda e: e.tensor_scalar(out=nxt[:], in0=nxt[:], scalar1=-PI, scalar2=PI,
                                                               op0=ALU.max, op1=ALU.min))(), r=[nxt], w=[nxt])
        S.add("act", (lambda nxt=nxt: lambda e: e.activation(out=nxt[:], in_=nxt[:], func=AF.Sin))(), r=[nxt], w=[nxt])
        cur = nxt
        kdim = 64
    z3 = cur
    pi = 0
    for lc in range(16):
        S.add("act", (lambda lc=lc: lambda e: e.activation(out=win[:], in_=delta[:], func=AF.Exp, scale=negt[:, lc:lc + 1]))(),
              r=[delta, negt], w=[win])
        for q in range(4):
            f_ = fr[q]
            for nb in range(4):
                p_ = ps[pi % 8]
                pi += 1
                S.add("pe", (lambda p_=p_, q=q, nb=nb, lc=lc: lambda e: e.matmul(
                    p_[:, :], lhsT=z3[:, lc * 128:(lc + 1) * 128], rhs=w4[:, q * D + nb * 512:q * D + (nb + 1) * 512],
                    start=True, stop=True))(), r=[z3, w4], w=[p_])
                S.add("dve", (lambda p_=p_, f_=f_, nb=nb: lambda e: e.tensor_tensor(
                    out=f_[:, nb * 512:(nb + 1) * 512], in0=p_[:, :], in1=win[:, nb * 512:(nb + 1) * 512], op=ALU.mult))(),
                    r=[p_, win], w=[f_])
        for o in range(2):
            hf, hb = fr[o], fr[2 + o]
            if lc == 0:
                S.add("pool", (lambda hb=hb: lambda e: e.memset(hb[0:1, :], 0.0))(), w=[hb])
            s_, d_ = osd[o], osd[2 + o]
            S.add("pool", (lambda hf=hf, hb=hb, s_=s_: lambda e: e.tensor_tensor(out=s_[:], in0=hf[:], in1=hb[:], op=ALU.add))(),
                  r=[hf, hb], w=[s_])
            S.add("pool", (lambda hf=hf, hb=hb, d_=d_: lambda e: e.tensor_tensor(out=d_[:], in0=hf[:], in1=hb[:], op=ALU.subtract))(),
                  r=[hf, hb], w=[d_])
            S.dma("sp", HS[o][lc * 128:(lc + 1) * 128, :], s_[:], r=[s_])
            S.dma("act", HD[o][lc * 128:(lc + 1) * 128, :], d_[:], r=[d_])
    C.close()


def dft_pass(nc, S, C, Xdram, FTp, rcs, epi, nyq=None):
    Xs = C.sb([128, 16, D], BF16, "dX")
    wt = [C.sb([128, 16, 128], BF16, "dW") for _ in range(2)]
    ps = [C.ps() for _ in range(8)]
    xv = Xdram.rearrange("(tc p) d -> p tc d", p=128)
    S.dma("sp", Xs[:, 0:8, :], xv[:, 0:8, :], w=[Xs])
    S.dma("act", Xs[:, 8:16, :], xv[:, 8:16, :], w=[Xs])
    pi = 0
    for it, rc in enumerate(rcs):
        w_ = wt[it % 2]
        S.dma("sp", w_[:], FTp[rc], w=[w_])
        pb = []
        for nb in range(4):
            p_ = ps[pi % 8]
            pi += 1
            pb.append(p_)
            for tc in range(16):
                S.add("pe", (lambda p_=p_, w_=w_, tc=tc, nb=nb: lambda e: e.matmul(
                    p_[:, :], lhsT=w_[:, tc, :], rhs=Xs[:, tc, nb * 512:(nb + 1) * 512], start=(tc == 0), stop=(tc == 15)))(),
                    r=[w_, Xs], w=[p_])
        epi(it, rc, pb)
    if nyq is not None:
        w_ = wt[len(rcs) % 2]
        S.dma("sp", w_[:], FTp[16], w=[w_])
        pb = []
        for nb in range(4):
            p_ = ps[pi % 8]
            pi += 1
            pb.append(p_)
            for tc in range(16):
                S.add("pe", (lambda p_=p_, w_=w_, tc=tc, nb=nb: lambda e: e.matmul(
                    p_[0:1, :], lhsT=w_[:, tc, 0:1], rhs=Xs[:, tc, nb * 512:(nb + 1) * 512], start=(tc == 0), stop=(tc == 15)))(),
                    r=[w_, Xs], w=[p_])
        nyq(pb)


def phase_hy_spec(nc, S, K, HS, HD, FTp, KH):
    for o in range(2):
        for (X, rcs, isre) in ((HD[o], list(range(16, 32)), False), (HS[o], list(range(0, 16)), True)):
            C = Ctx(nc, S)
            ot = [C.sb([128, D], F32, "ko") for _ in range(2)]
            nrow = C.sb([1, D], F32, "nr")

            def epi(it, rc, pb, o=o, ot=ot):
                o_ = ot[it % 2]
                for nb in range(4):
                    if nb % 2:
                        S.add("act", (lambda nb=nb: lambda e: e.activation(out=o_[:, nb * 512:(nb + 1) * 512], in_=pb[nb][:, :],
                                                                           func=AF.Copy))(), r=[pb[nb]], w=[o_])
                    else:
                        S.add("dve", (lambda nb=nb: lambda e: e.tensor_copy(out=o_[:, nb * 512:(nb + 1) * 512], in_=pb[nb][:, :]))(),
                              r=[pb[nb]], w=[o_])
                S.dma("pool", KH[o][rc * 128:(rc + 1) * 128, :], o_[:], r=[o_])

            def nyq(pb, o=o, nrow=nrow):
                for nb in range(4):
                    S.add("dve", (lambda nb=nb: lambda e: e.tensor_copy(out=nrow[0:1, nb * 512:(nb + 1) * 512], in_=pb[nb][0:1, :]))(),
                          r=[pb[nb]], w=[nrow])
                S.dma("pool", KH[o][2048:2049, :], nrow[:], r=[nrow])

            dft_pass(nc, S, C, X, FTp, rcs, epi, nyq if isre else None)
            C.close()


def phase_hy_conv(nc, S, K, o, Xin, KH, FTp, Gp, YH, gate, skip_row, Yout_tm, Yout_fm):
    C = Ctx(nc, S)
    vr, vi = C.sb([128, D]), C.sb([128, D])
    kr, ki = C.sb([128, D]), C.sb([128, D])
    t1, t2 = C.sb([128, D]), C.sb([128, D])
    yr = [C.sb([128, D], BF16) for _ in range(2)]
    yi = [C.sb([128, D], BF16) for _ in range(2)]
    rcs = []
    for j in range(16):
        rcs += [j, 16 + j]

    def epi(it, rc, pb):
        j = rc % 16
        dst = vr if rc < 16 else vi
        kd_ = kr if rc < 16 else ki
        S.dma("act", kd_[:], KH[rc * 128:(rc + 1) * 128, :], w=[kd_])
        for nb in range(4):
            S.add("act", (lambda nb=nb: lambda e: e.activation(out=dst[:, nb * 512:(nb + 1) * 512], in_=pb[nb][:, :],
                                                               func=AF.Copy))(), r=[pb[nb]], w=[dst])
        if rc >= 16:
            yr_, yi_ = yr[j % 2], yi[j % 2]
            S.add("dve", lambda e: e.tensor_tensor(out=t1[:], in0=vr[:], in1=kr[:], op=ALU.mult), r=[vr, kr], w=[t1])
            S.add("pool", lambda e: e.tensor_tensor(out=t2[:], in0=vi[:], in1=ki[:], op=ALU.mult), r=[vi, ki], w=[t2])
            S.add("dve", lambda e: e.tensor_tensor(out=yr_[:], in0=t1[:], in1=t2[:], op=ALU.subtract), r=[t1, t2], w=[yr_])
            S.add("pool", lambda e: e.tensor_tensor(out=t1[:], in0=vr[:], in1=ki[:], op=ALU.mult), r=[vr, ki], w=[t1])
            S.add("dve", lambda e: e.tensor_tensor(out=t2[:], in0=vi[:], in1=kr[:], op=ALU.mult), r=[vi, kr], w=[t2])
            S.add("pool", lambda e: e.tensor_tensor(out=yi_[:], in0=t1[:], in1=t2[:], op=ALU.add), r=[t1, t2], w=[yi_])
            if j == 0:
                S.add("dve", lambda e: e.tensor_tensor(out=yr_[0:1, :], in0=vr[0:1, :], in1=kr[0:1, :], op=ALU.mult),
                      r=[vr, kr, yr_], w=[yr_])
                S.add("dve", lambda e: e.tensor_tensor(out=yi_[0:1, :], in0=vi[0:1, :], in1=ki[0:1, :], op=ALU.mult),
                      r=[vi, ki, yi_], w=[yi_])
            S.dma("pool", YH[j * 128:(j + 1) * 128, :], yr_[:], r=[yr_])
            S.dma("pool", YH[(16 + j) * 128:(17 + j) * 128, :], yi_[:], r=[yi_])

    dft_pass(nc, S, C, Xin, FTp, rcs, epi)
    C.close()
    C = Ctx(nc, S)
    HW = D // 2
    Ys = C.sb([128, 32, HW], BF16, "iY")
    gw = [C.sb([128, 32, 128], BF16, "iW") for _ in range(2)]
    gt = [C.sb([128, HW], BF16) for _ in range(2)]
    yp = [C.sb([128, HW], BF16) for _ in range(2)]
    tt = [C.sb([128, HW]) for _ in range(2)]
    yo = [C.sb([128, HW], BF16) for _ in range(2)]
    skb = C.sb([128, D])
    S.dma("sp", skb[:], skip_row.partition_broadcast(128), w=[skb])
    ps = [C.ps() for _ in range(4)]
    if Yout_fm is not None:
        pH = C.ps([128, 1024], BF16)
        oT = [C.sb([128, 8, 128], BF16) for _ in range(2)]
        yfv = Yout_fm.rearrange("(c p) t -> p c t", p=128)
    yv = YH.rearrange("(rc p) d -> p rc d", p=128)
    it = 0
    pi = 0
    for hf in range(2):
        cs_ = slice(hf * HW, (hf + 1) * HW)
        S.dma("sp", Ys[:, 0:16, :], yv[:, 0:16, cs_], w=[Ys])
        S.dma("act", Ys[:, 16:32, :], yv[:, 16:32, cs_], w=[Ys])
        for tcn in range(16):
            w_, g_, p__, t_, o_ = gw[it % 2], gt[it % 2], yp[it % 2], tt[it % 2], yo[it % 2]
            S.dma("sp", w_[:], Gp[tcn], w=[w_])
            S.dma("act", g_[:], gate[tcn * 128:(tcn + 1) * 128, cs_], w=[g_])
            S.dma("act", p__[:], Xin[tcn * 128:(tcn + 1) * 128, cs_], w=[p__])
            S.add("pool", (lambda t_=t_, p__=p__, cs_=cs_: lambda e: e.tensor_tensor(out=t_[:], in0=p__[:], in1=skb[:, cs_], op=ALU.mult))(),
                  r=[p__, skb], w=[t_])
            for nb in range(2):
                p_ = ps[pi % 4]
                pi += 1
                for rc in range(32):
                    S.add("pe", (lambda p_=p_, w_=w_, rc=rc, nb=nb: lambda e: e.matmul(
                        p_[:, :], lhsT=w_[:, rc, :], rhs=Ys[:, rc, nb * 512:(nb + 1) * 512], start=(rc == 0), stop=(rc == 31)))(),
                        r=[w_, Ys], w=[p_])
                S.add("dve", (lambda p_=p_, t_=t_, nb=nb: lambda e: e.tensor_tensor(
                    out=t_[:, nb * 512:(nb + 1) * 512], in0=p_[:, :], in1=t_[:, nb * 512:(nb + 1) * 512], op=ALU.add))(),
                    r=[p_, t_], w=[t_])
            S.add("pool", (lambda t_=t_, g_=g_, o_=o_: lambda e: e.tensor_tensor(out=o_[:], in0=t_[:], in1=g_[:], op=ALU.mult))(),
                  r=[t_, g_], w=[o_])
            if Yout_tm is not None:
                S.dma("pool", Yout_tm[tcn * 128:(tcn + 1) * 128, cs_], o_[:], r=[o_])
            if Yout_fm is not None:
                oT_ = oT[it % 2]
                for c in range(8):
                    S.add("pe", (lambda c=c, o_=o_: lambda e: e.transpose(out=pH[:, c * 128:(c + 1) * 128], in_=o_[:, c * 128:(c + 1) * 128],
                                                                          identity=K["identB"][:]))(), r=[o_, K["identB"]], w=[pH])
                S.add("act", (lambda oT_=oT_: lambda e: e.activation(out=oT_[:], in_=pH[:, :], func=AF.Copy))(), r=[pH], w=[oT_])
                S.dma("pool", yfv[:, hf * 8:(hf + 1) * 8, tcn * 128:(tcn + 1) * 128], oT_[:], r=[oT_])
            it += 1
    C.close()


def build(dbg=(), upto=99, nsteps=18, lvl=9):
    nc = bass.Bass("TRN2", target_bir_lowering=False)

    def din(name, shape, dt=F32):
        return nc.dram_tensor(name, list(shape), dt, kind="ExternalInput").ap()

    def dint(name, shape, dt=F32):
        kind = "ExternalOutput" if name in dbg else "Internal"
        return nc.dram_tensor(name, list(shape), dt, kind=kind).ap()

    cc = din("cc", [128, 16, 2])
    ada_w = din("ada_w", [2, D, 6 * D])
    ada_b = din("ada_b", [2, 6 * D])
    x = din("x", [T, D])
    ctx = din("ctx", [TC, D])
    norm_g = din("norm_g", [2, 4, D])
    identF = din("identF", [128, 128])
    identB = din("identB", [128, 128], BF16)
    bones = din("bones", [128, 128])
    cols = din("cols", [128, len(COLS), 16])
    mu = din("mu", [128, 16, 6])
    P = {}
    for n, shp in (("rw_w_r", [16, 128, 16, 128]), ("rw_w_k", [16, 128, 16, 128]), ("rw_w_v", [16, 128, 16, 128]),
                   ("rw_w_o", [16, 128, 16, 128]),
                   ("rw_dec_w1", [2, 1, 128, 16, 96]), ("rw_dec_w2", [2, 96, D]), ("rw_a1", [2, 1, 128, 16, 96]), ("rw_a2", [2, 96, D]),
                   ("rw_g1", [2, 128, 16, 128]), ("rw_g2", [16, 128, 2, 128])):
        P[n] = din(n, shp)
    modv = dint("modv", [2, 2, 6 * D])
    h_fm = dint("h_fm", [D, TT], BF16)
    xm = [dint("xm%d" % j, [D, TT], BF16) for j in range(6)]
    Dr = {"r": dint("r_fm", [D, TT]), "k": dint("k_fm", [D, TT]), "v": dint("v_fm", [D, TT]),
          "t1": dint("t1", [96, TT], BF16), "t2": dint("t2", [256, TT], BF16),
          "logw": [dint("logw%d" % e, [D, TT]) for e in range(2)],
          "a": [dint("a%d" % e, [D, TT]) for e in range(2)], "g": dint("g_fm", [D, TT]),
          "kkn": dint("kkn", [D, TT]), "kd": [dint("kd%d" % e, [D, TT]) for e in range(2)],
          "b": [dint("b%d" % e, [D, TT]) for e in range(2)], "bonus": dint("bonus", [D, TT])}
    Odram = [dint("o%d" % e, [D, T]) for e in range(2)]
    scan_consts = {"mask01": din("mask01", [128, 16, 128]), "mask": [din("mask_%d" % e, [128, 256]) for e in range(2)],
                   "maskT": [din("maskT_%d" % e, [128, 128]) for e in range(2)], "id2": din("id2", [128, 64]),
                   "md": [din("md_%d" % e, [128, 128]) for e in range(2)], "m1c": [din("m1c_%d" % e, [128, 128]) for e in range(2)],
                   "m2c": [din("m2c_%d" % e, [128, 128]) for e in range(2)], "mtd": [din("mtd_%d" % e, [128, 128]) for e in range(2)]}
    XO = dint("XO", [D, T], BF16)
    YFM = dint("YFM", [D, T])
    XA, XB, XC = dint("XA", [T, D]), dint("XB", [T, D]), dint("XC", [T, D])
    H2, H1 = dint("H2", [D, T], BF16), dint("H1", [D, T], BF16)
    HID = dint("HID", [DFF, T], BF16)
    mlp_up, mlp_down = din("mlp_up", [2, 64, 128, 16, 128]), din("mlp_down", [2, 16, 128, 64, 128])
    hy_in_w, hy_out_w = din("hy_in_w", [48, 128, 16, 128]), din("hy_out_w", [16, 128, 16, 128])
    hy_skip = din("hy_skip", [2, D])
    HC = {"z0T": din("z0T", [33, T]), "f_w1": din("f_w1", [33, 64]), "f_w23": din("f_w23", [2, 64, 64]),
          "f_w4": din("f_w4", [64, 4 * D]), "fbq": din("fbq", [64, 6]), "delta": din("delta", [D]), "negt": din("negt", [128, 16])}
    FTp = din("FTp", [32, 128, 16, 128], BF16)
    Gp = din("Gp", [16, 128, 32, 128], BF16)
    TM3 = [dint("TM3_%d" % i, [T, D], BF16) for i in range(3)]
    HS = [dint("HS%d" % o, [T, D], BF16) for o in range(2)]
    HD = [dint("HD%d" % o, [T, D], BF16) for o in range(2)]
    KH = [dint("KH%d" % o, [2 * T, D]) for o in range(2)]
    YH = dint("YH", [2 * T, D], BF16)
    Y1T = dint("Y1T", [T, D], BF16)
    Y2FM = dint("Y2FM", [D, T], BF16)
    outp = nc.dram_tensor("out", [T, D], F32, kind="ExternalOutput").ap()
    mrow = lambda l, row, j: modv[l, row, j * D:(j + 1) * D]
    with ExitStack() as st:
        S = Sched(nc, st)
        phase_clear(nc, S)
        K = {}
        for n, shp, dt in (("identF", [128, 128], F32), ("identB", [128, 128], BF16), ("bones", [128, 128], F32),
                           ("cols", [128, len(COLS), 16], F32), ("mu", [128, 16, 6], F32), ("omu", [128, 16, 6], F32)):
            K[n] = st.enter_context(nc.sbuf_tensor("K_" + n, shp, dt))
        phase_consts(nc, S, K, identF, identB, bones, cols, mu)
        phase_ada(nc, S, cc, ada_w, ada_b, modv)
        if upto >= 1:
            mrow = lambda l, row, j: modv[l, row, j * D:(j + 1) * D]
            phase_norm(nc, S, K, ctx, TC, None, None, h_fm, (0, TC + T), None,
                       (norm_g[0, 0], mrow(0, 1, 1), mrow(0, 1, 0)))
            phase_norm(nc, S, K, x, T, None, None, h_fm, (TC,), None,
                       (norm_g[0, 0], mrow(0, 0, 1), mrow(0, 0, 0)))
        if upto >= 2:
            phase_mix(nc, S, K, h_fm, xm)
        if upto >= 3:
            phase_rwkv_proj(nc, S, K, xm, P, Dr)
        if upto >= 4:
            phase_scanprep(nc, S, K, Dr)
        if upto >= 5:
            phase_scan(nc, S, K, Dr, Odram, scan_consts, TDT=F32, nsteps=nsteps, lvl=lvl)
        if upto >= 6:
            colf = lambda name: (lambda m: K["cols"][:, CI[name], m:m + 1])
            phase_readout(nc, S, K, Dr, Odram, XO)
            C = Ctx(nc, S)
            gemm_fm(nc, S, C, XO, D, T, P["rw_w_o"], D, epi_store(S, YFM))
            C.close()
            phase_norm(nc, S, K, x, T, YFM, XA, H2, (0,), (mrow(0, 0, 2), norm_g[0, 1]),
                       (norm_g[0, 2], mrow(0, 0, 4), mrow(0, 0, 3)))
            phase_mlp(nc, S, K, H2, mlp_up[0], mlp_down[0], HID, YFM)
            phase_norm(nc, S, K, XA, T, YFM, XB, H1, (0,), (mrow(0, 0, 5), norm_g[0, 3]),
                       (norm_g[1, 0], mrow(1, 0, 1), mrow(1, 0, 0)))
        if upto >= 7:
            phase_hy_in(nc, S, K, H1, hy_in_w, TM3)
            phase_hy_filt(nc, S, K, HC, HS, HD)
            phase_hy_spec(nc, S, K, HS, HD, FTp, KH)
            phase_hy_conv(nc, S, K, 0, TM3[0], KH[0], FTp, Gp, YH, TM3[1], hy_skip[0], Y1T, None)
            phase_hy_conv(nc, S, K, 1, Y1T, KH[1], FTp, Gp, YH, TM3[2], hy_skip[1], None, Y2FM)
            C = Ctx(nc, S)
            gemm_fm(nc, S, C, Y2FM, D, T, hy_out_w, D, epi_act(S, C, YFM, AF.Identity, T, odt=F32, bias=colf("hy_out_b")))
            C.close()
            phase_norm(nc, S, K, XB, T, YFM, XC, H2, (0,), (mrow(1, 0, 2), norm_g[1, 1]),
                       (norm_g[1, 2], mrow(1, 0, 4), mrow(1, 0, 3)))
            phase_mlp(nc, S, K, H2, mlp_up[1], mlp_down[1], HID, YFM)
            phase_norm(nc, S, K, XC, T, YFM, outp, None, (), (mrow(1, 0, 5), norm_g[1, 3]), None)
    return nc


def wtile(w, msz=128):
    w = np.asarray(w, np.float32)
    Kd, N = w.shape
    return np.ascontiguousarray(w.reshape(Kd // 128, 128, N // msz, msz).transpose(2, 1, 0, 3))


def tiled_weights(inputs):
    f = lambda a: np.ascontiguousarray(np.asarray(a, dtype=np.float32))
    m = {}
    for n in ("rw_w_r", "rw_w_k", "rw_w_v", "rw_w_o", "rw_g1", "rw_g2"):
        m[n] = wtile(inputs[n][0])
    for n in ("rw_dec_w1", "rw_a1"):
        m[n] = np.stack([wtile(inputs[n][0][e], 96) for e in range(2)], axis=0)
    for n in ("rw_dec_w2", "rw_a2"):
        m[n] = f(inputs[n][0])
    m["mlp_up"] = np.stack([wtile(inputs["mlp_up"][l]) for l in range(2)], axis=0)
    m["mlp_down"] = np.stack([wtile(inputs["mlp_down"][l]) for l in range(2)], axis=0)
    m["hy_in_w"] = wtile(inputs["hy_in_w"][0])
    m["hy_out_w"] = wtile(inputs["hy_out_w"][0])
    return m


def colpack(v):
    return np.asarray(v, np.float32).reshape(16, 128).T


def make_inputs(inputs, b, tw):
    f = lambda a: np.ascontiguousarray(np.asarray(a, dtype=np.float32))
    m = {}
    cc = np.zeros((128, 16, 2), np.float32)
    cc[:, :, 0] = colpack(inputs["c"][b])
    cc[:, :, 1] = colpack(inputs["c_ctx"])
    m["cc"] = cc
    m["ada_w"] = f(inputs["ada_w"])
    m["ada_b"] = f(inputs["ada_b"])
    m["x"] = f(inputs["x"][b])
    m["ctx"] = f(inputs["ctx"][b])
    m["norm_g"] = f(inputs["norm_g"])
    m["identF"] = np.eye(128, dtype=np.float32)
    m["identB"] = np.eye(128).astype(ml_dtypes.bfloat16)
    bo = np.zeros((128, 128), np.float32)
    bo[:64, :64] = 1.0
    bo[64:, 64:] = 1.0
    m["bones"] = bo
    cv = {"k_k": inputs["rw_k_k"][0], "k_a": inputs["rw_k_a"][0], "r_k": np.asarray(inputs["rw_r_k"][0]).reshape(-1),
          "lnx_w": inputs["rw_lnx_w"][0], "lnx_b": inputs["rw_lnx_b"][0],
          "dec_w0_0": inputs["rw_dec_w0"][0, 0], "dec_w0_1": inputs["rw_dec_w0"][0, 1],
          "a0_0": inputs["rw_a0"][0, 0], "a0_1": inputs["rw_a0"][0, 1], "hy_out_b": inputs["hy_out_b"][0]}
    for s in range(3):
        cv["hy_in_b%d" % s] = inputs["hy_in_b"][0, s * D:(s + 1) * D]
        cv["cb_%d" % s] = inputs["hy_conv_b"][0, s * D:(s + 1) * D]
        for tap in range(3):
            cv["cw%d_%d" % (tap, s)] = inputs["hy_conv_w"][0, tap, s * D:(s + 1) * D]
    colsarr = np.zeros((128, len(COLS), 16), np.float32)
    for n in COLS:
        colsarr[:, CI[n], :] = colpack(cv[n])
    m["cols"] = colsarr
    mu = np.asarray(inputs["rw_mu"][0], np.float32)
    m["mu"] = np.ascontiguousarray(mu.reshape(6, 16, 128).transpose(2, 1, 0))
    m01 = np.ones((128, 16, 128), np.float32)
    m01[:, :, 0] = 0.0
    m["mask01"] = m01
    ii = np.arange(128)[:, None]
    tt = np.arange(128)[None, :]
    m["mask_0"] = np.concatenate([(ii < tt), (ii <= tt)], axis=1).astype(np.float32)
    m["mask_1"] = np.concatenate([(ii > tt), (ii >= tt)], axis=1).astype(np.float32)
    m["maskT_0"] = (ii > tt).astype(np.float32)
    m["maskT_1"] = (ii < tt).astype(np.float32)
    b32 = (ii // 32) == (tt // 32)
    b64 = (ii // 64) == (tt // 64)
    for e_, st_ in ((0, ii < tt), (1, ii > tt)):
        m["md_%d" % e_] = (st_ & b32).astype(np.float32)
        m["m1c_%d" % e_] = (st_ & b64 & ~b32).astype(np.float32)
        m["m2c_%d" % e_] = (st_ & ~b64).astype(np.float32)
        m["mtd_%d" % e_] = (st_.T & b32).astype(np.float32)
    id2 = np.zeros((128, 64), np.float32)
    id2[np.arange(128), np.arange(128) % 64] = 1.0
    m["id2"] = id2
    m.update(tw)
    m["hy_skip"] = f(inputs["hy_skip"][0])
    m["f_w1"] = f(inputs["hy_f_w1"][0])
    m["f_w23"] = f(inputs["hy_f_w23"][0])
    m["f_w4"] = f(inputs["hy_f_w4"][0])
    m["fbq"] = np.ascontiguousarray(np.concatenate([np.asarray(inputs["hy_f_b"][0], np.float32).T,
                                                    np.asarray(inputs["hy_f_freq"][0], np.float32).T], axis=1))
    m.update(HCONST)
    return m


def _hconst():
    L_ = T
    t = np.linspace(0.0, 1.0, L_, dtype=np.float32)
    freqs = np.linspace(1e-4, 15.0, 16, dtype=np.float32)[None, :]
    ang = (np.float32(2.0 * np.pi / L_) * np.arange(L_, dtype=np.float32)[:, None]) * freqs
    z0 = np.concatenate([t[:, None], np.cos(ang), -np.sin(ang)], axis=-1).astype(np.float32)
    c = {"z0T": np.ascontiguousarray(z0.T)}
    max_decay = np.log(1e-2) / 0.3
    min_decay = np.log(1e-2) / 1.5
    c["delta"] = np.abs(np.linspace(min_decay, max_decay, D, dtype=np.float32)).astype(np.float32)
    c["negt"] = np.ascontiguousarray((-t).reshape(16, 128).T)
    tt = np.arange(T, dtype=np.float64)[:, None]
    ff = np.arange(2048, dtype=np.float64)[None, :]
    th = 2.0 * np.pi * ff * tt / 4096.0
    FT = np.concatenate([np.cos(th), -np.sin(th)], axis=1)
    FT[:, 2048] = (-1.0) ** np.arange(T)
    sc = np.full(4096, 2.0 / 4096.0)
    sc[0] = 1.0 / 4096.0
    sc[2048] = 1.0 / 4096.0
    G = (FT * sc[None, :]).T
    c["FTp"] = np.ascontiguousarray(FT.reshape(16, 128, 32, 128).transpose(2, 1, 0, 3)).astype(ml_dtypes.bfloat16)
    c["Gp"] = np.ascontiguousarray(G.reshape(32, 128, 16, 128).transpose(2, 1, 0, 3)).astype(ml_dtypes.bfloat16)
    return c


HCONST = _hconst()


def run(inputs, dbg=(), cores=NCORES, trace=False, upto=99, nsteps=18, lvl=9):
    nc = build(dbg, upto, nsteps, lvl)
    tw = tiled_weights(inputs)
    in_maps = [make_inputs(inputs, b, tw) for b in range(cores)]
    res = run_bass_kernel_spmd(nc, in_maps, core_ids=list(range(cores)), trace=trace)
    return res


def kernel(**inputs):
    res = run(inputs)
    out = np.stack([np.asarray(r["out"]) for r in res.results], axis=0)
    return out.astype(np.float32)
```
